# Optimizing a Trainium2 kernel written in Bass

```python
import math
import jax, jax.numpy as jnp
from jax import lax
import numpy as np

D_MODEL = 1024
BATCH = 16
SEQ = 2048
DEPTH = 4

D_MIX = D_MODEL
D_ATTN = D_MIX // 2
D_REC = D_MIX - D_ATTN
HEAD_DIM = 64
N_ATTN_HEADS = D_ATTN // HEAD_DIM
N_REC_BLOCKS = 8
REC_BLOCK = D_REC // N_REC_BLOCKS
CONV_WIDTH = 4
LRU_C = 8.0
DILATED_PATTERNS = ((128, 1), (512, 4), (2048, 16))
ROPE_THETA = 10000.0
D_FF = 256 * ((8 * D_MODEL // 3 + 255) // 256)
D_IN_PROJ = 3 * D_ATTN + 2 * D_REC
EPS = 1e-6

kernel_name = "hymba_rglru_dilated_attn_macaron"


def rms_norm(x, g):
    xf = x.astype(jnp.float32)
    var = jnp.mean(xf * xf, axis=-1, keepdims=True)
    return (xf * lax.rsqrt(var + EPS) * g.astype(jnp.float32)).astype(x.dtype)


def swiglu_ffn(x, g, w_in, w_out):
    h = rms_norm(x, g) @ w_in
    gate, up = jnp.split(h, 2, axis=-1)
    return (jax.nn.silu(gate) * up) @ w_out


def rope_tables(seq):
    pos = jnp.arange(seq, dtype=jnp.float32)
    inv = ROPE_THETA ** (-jnp.arange(0, HEAD_DIM, 2, dtype=jnp.float32) / HEAD_DIM)
    ang = pos[:, None] * inv[None, :]
    return jnp.cos(ang), jnp.sin(ang)


def apply_rope(t, cos, sin):
    tf = t.astype(jnp.float32)
    half = HEAD_DIM // 2
    t1, t2 = tf[..., :half], tf[..., half:]
    c = cos[None, :, None, :]
    s = sin[None, :, None, :]
    return jnp.concatenate([t1 * c - t2 * s, t2 * c + t1 * s], axis=-1)


def dilated_band_attention(q, k, v, window, dilation):
    B, S, H, Dh = q.shape
    steps = window // dilation
    L = S // dilation
    nb = -(-L // steps)
    Lp = nb * steps

    def to_classes(t):
        return t.reshape(B, L, dilation, H, Dh).transpose(0, 2, 3, 1, 4)

    qc, kc, vc = to_classes(q), to_classes(k), to_classes(v)
    qc = jnp.pad(qc, ((0, 0), (0, 0), (0, 0), (0, Lp - L), (0, 0)))
    kc = jnp.pad(kc, ((0, 0), (0, 0), (0, 0), (steps, Lp - L), (0, 0)))
    vc = jnp.pad(vc, ((0, 0), (0, 0), (0, 0), (steps, Lp - L), (0, 0)))

    def band_keys(t):
        prev = t[:, :, :, :Lp].reshape(B, dilation, H, nb, steps, Dh)
        cur = t[:, :, :, steps:].reshape(B, dilation, H, nb, steps, Dh)
        return jnp.concatenate([prev, cur], axis=-2)

    q_blk = qc.reshape(B, dilation, H, nb, steps, Dh)
    k_blk, v_blk = band_keys(kc), band_keys(vc)

    scores = jnp.einsum('bdhnqc,bdhnkc->bdhnqk', q_blk, k_blk,
                        preferred_element_type=jnp.float32) / math.sqrt(Dh)
    qi = jnp.arange(steps)[:, None]
    kj = jnp.arange(2 * steps)[None, :]
    dist = qi + steps - kj
    key_idx = jnp.arange(nb)[:, None, None] * steps - steps + kj[None]
    valid = (dist >= 0)[None] & (dist <= steps)[None] & (key_idx >= 0)
    scores = jnp.where(valid, scores, -jnp.inf)
    m = jnp.max(scores, axis=-1, keepdims=True)
    p = jnp.exp(scores - m)
    s = jnp.sum(p, axis=-1, keepdims=True)
    num = jnp.einsum('bdhnqk,bdhnkc->bdhnqc', p, v_blk.astype(jnp.float32))

    def from_classes(t):
        c = t.shape[-1]
        t = t.reshape(B, dilation, H, Lp, c)[:, :, :, :L]
        return t.transpose(0, 3, 1, 2, 4).reshape(B, S, H, c)

    return from_classes(num), from_classes(m), from_classes(s)


def dilated_attention_group(q, k, v):
    nums, ms, ss = [], [], []
    for window, dilation in DILATED_PATTERNS:
        n_, m_, s_ = dilated_band_attention(q, k, v, window, dilation)
        nums.append(n_); ms.append(m_); ss.append(s_)
    m_all = jnp.maximum(jnp.maximum(ms[0], ms[1]), ms[2])
    ws = [jnp.exp(m_ - m_all) for m_ in ms]
    numer = nums[0] * ws[0] + nums[1] * ws[1] + nums[2] * ws[2]
    denom = ss[0] * ws[0] + ss[1] * ws[1] + ss[2] * ws[2]
    return numer / denom


def causal_depthwise_conv(x, w, b):
    C = x.shape[-1]
    out = lax.conv_general_dilated(
        x, w[:, None, :].astype(x.dtype), window_strides=(1,),
        padding=[(CONV_WIDTH - 1, 0)], dimension_numbers=('NWC', 'WIO', 'NWC'),
        feature_group_count=C)
    return out + b


def block_diag_linear(x, w, b):
    B, S, _ = x.shape
    xb = x.reshape(B, S, N_REC_BLOCKS, REC_BLOCK)
    return jnp.einsum('bsgi,gij->bsgj', xb, w).reshape(B, S, D_REC) + b


def rglru_group(xb, gb, conv_w, conv_b, w_a, b_a, w_x, b_x, lam):
    xr = causal_depthwise_conv(xb, conv_w, conv_b).astype(jnp.float32)
    r = jax.nn.sigmoid(block_diag_linear(xr, w_a.astype(jnp.float32), b_a.astype(jnp.float32)))
    i = jax.nn.sigmoid(block_diag_linear(xr, w_x.astype(jnp.float32), b_x.astype(jnp.float32)))
    log_a = -LRU_C * r * jax.nn.softplus(-lam.astype(jnp.float32))
    a = jnp.exp(log_a)
    u = jnp.sqrt(-jnp.expm1(2.0 * log_a)) * (i * xr)

    def combine(c1, c2):
        a1, b1 = c1
        a2, b2 = c2
        return a1 * a2, a2 * b1 + b2

    _, h = lax.associative_scan(combine, (a, u), axis=1)
    return h * jax.nn.gelu(gb.astype(jnp.float32))


def hybrid_mixer(x, norm_g, w_in, conv_w, conv_b, w_a, b_a, w_x, b_x, lam,
                 attn_out_g, rec_out_g, w_out, cos, sin):
    B, S, _ = x.shape
    h = rms_norm(x, norm_g)
    proj = h @ w_in
    q, k, v, xb, gb = jnp.split(
        proj, [D_ATTN, 2 * D_ATTN, 3 * D_ATTN, 3 * D_ATTN + D_REC], axis=-1)
    q = apply_rope(q.reshape(B, S, N_ATTN_HEADS, HEAD_DIM), cos, sin)
    k = apply_rope(k.reshape(B, S, N_ATTN_HEADS, HEAD_DIM), cos, sin)
    v = v.reshape(B, S, N_ATTN_HEADS, HEAD_DIM).astype(jnp.float32)
    y_attn = dilated_attention_group(q, k, v).reshape(B, S, D_ATTN)
    y_rec = rglru_group(xb, gb, conv_w, conv_b, w_a, b_a, w_x, b_x, lam)
    merged = jnp.concatenate(
        [rms_norm(y_attn, attn_out_g), rms_norm(y_rec, rec_out_g)], axis=-1).astype(x.dtype)
    return merged @ w_out


def setup_inputs(seed: int = 0) -> dict:
    key = jax.random.key(seed)
    ks = jax.random.split(key, 20)
    L = DEPTH

    def nrm(k, shape, scale):
        return jax.random.normal(k, shape, jnp.float32) * scale

    def gain(k, shape):
        return 1.0 + 0.02 * jax.random.normal(k, shape, jnp.float32)

    a0 = jax.random.uniform(ks[12], (L, D_REC), jnp.float32, minval=0.9, maxval=0.999)
    return {
        "x": nrm(ks[0], (BATCH, SEQ, D_MODEL), 1.0),
        "ffn1_norm": gain(ks[1], (L, D_MODEL)),
        "ffn1_w_in": nrm(ks[2], (L, D_MODEL, 2 * D_FF), D_MODEL ** -0.5),
        "ffn1_w_out": nrm(ks[3], (L, D_FF, D_MODEL), D_FF ** -0.5),
        "mix_norm": gain(ks[4], (L, D_MODEL)),
        "w_in": nrm(ks[5], (L, D_MODEL, D_IN_PROJ), D_MODEL ** -0.5),
        "conv_w": nrm(ks[6], (L, CONV_WIDTH, D_REC), CONV_WIDTH ** -0.5),
        "conv_b": nrm(ks[7], (L, D_REC), 0.01),
        "rg_w_a": nrm(ks[8], (L, N_REC_BLOCKS, REC_BLOCK, REC_BLOCK), REC_BLOCK ** -0.5),
        "rg_b_a": nrm(ks[9], (L, D_REC), 0.01),
        "rg_w_x": nrm(ks[10], (L, N_REC_BLOCKS, REC_BLOCK, REC_BLOCK), REC_BLOCK ** -0.5),
        "rg_b_x": nrm(ks[11], (L, D_REC), 0.01),
        "rg_lambda": jnp.log(a0) - jnp.log1p(-a0),
        "attn_out_norm": gain(ks[13], (L, D_ATTN)),
        "rec_out_norm": gain(ks[14], (L, D_REC)),
        "w_out": nrm(ks[15], (L, D_MIX, D_MODEL), D_MIX ** -0.5),
        "ffn2_norm": gain(ks[16], (L, D_MODEL)),
        "ffn2_w_in": nrm(ks[17], (L, D_MODEL, 2 * D_FF), D_MODEL ** -0.5),
        "ffn2_w_out": nrm(ks[18], (L, D_FF, D_MODEL), D_FF ** -0.5),
        "final_norm": gain(ks[19], (D_MODEL,)),
    }


def reference(x, ffn1_norm, ffn1_w_in, ffn1_w_out, mix_norm, w_in, conv_w, conv_b,
              rg_w_a, rg_b_a, rg_w_x, rg_b_x, rg_lambda, attn_out_norm, rec_out_norm,
              w_out, ffn2_norm, ffn2_w_in, ffn2_w_out, final_norm):
    cos, sin = rope_tables(x.shape[1])
    for l in range(DEPTH):
        x = x + 0.5 * swiglu_ffn(x, ffn1_norm[l], ffn1_w_in[l], ffn1_w_out[l])
        x = x + hybrid_mixer(x, mix_norm[l], w_in[l], conv_w[l], conv_b[l],
                             rg_w_a[l], rg_b_a[l], rg_w_x[l], rg_b_x[l], rg_lambda[l],
                             attn_out_norm[l], rec_out_norm[l], w_out[l], cos, sin)
        x = x + 0.5 * swiglu_ffn(x, ffn2_norm[l], ffn2_w_in[l], ffn2_w_out[l])
    return rms_norm(x, final_norm)
```

```python
import numpy as np
from contextlib import ExitStack
import concourse.bass as bass
import concourse.mybir as mybir
from concourse.bass_utils import run_bass_kernel_spmd

F32 = mybir.dt.float32
BF16 = mybir.dt.bfloat16
AF = mybir.ActivationFunctionType
ALU = mybir.AluOpType

D = 1024
T = 2048
TT = 512
NTT = 4
KC = 8
DFF = 2816
NFC = 22
DEPTH = 4
EPS = 1e-6
UPL = 161
NVL = 64
MASKW = 2432
GS = 2
NU = 12

ENGS = ['pe', 'act', 'dve', 'pool', 'sp']
BLOCK_NAME = {'pe': 'tensor', 'act': 'scalar', 'dve': 'vector', 'pool': 'gpsimd', 'sp': 'sync'}


class Op:
    __slots__ = ('eng', 'fn', 'idx', 'is_dma', 'deps', 'ms', 'needs_inc', 'dma_n', 'dma_sem', 'dma_val')


class Sched:
    def __init__(self, dma_ring=8):
        self.ops = {e: [] for e in ENGS}
        self.last_w = {}
        self.readers = {}
        self.dma_count = {e: 0 for e in ENGS}
        self.R = dma_ring

    def add(self, eng, fn, reads=(), writes=(), dma=False):
        op = Op()
        op.eng = eng
        op.fn = fn
        op.idx = len(self.ops[eng])
        op.is_dma = dma
        op.needs_inc = False
        op.ms = 0
        deps = {}
        for r in reads:
            w = self.last_w.get(r)
            if w is not None:
                deps[id(w)] = w
            if type(r) is tuple and r[0] == 'ps':
                rd = self.readers.get(r)
                if rd:
                    for kk, o in rd.items():
                        if kk != 'dma' and kk != eng:
                            deps[id(o)] = o
        for k in writes:
            w = self.last_w.get(k)
            if w is not None:
                deps[id(w)] = w
            rd = self.readers.get(k)
            if rd:
                for kk, o in rd.items():
                    if kk == 'dma':
                        for oo in o:
                            deps[id(oo)] = oo
                    else:
                        deps[id(o)] = o
        op.deps = list(deps.values())
        for r in reads:
            rd = self.readers.setdefault(r, {})
            if dma:
                rd.setdefault('dma', []).append(op)
            else:
                rd[eng] = op
        for k in writes:
            self.last_w[k] = op
            self.readers[k] = {}
        if dma:
            op.dma_n = self.dma_count[eng]
            self.dma_count[eng] += 1
        self.ops[eng].append(op)
        return op

    def pe(self, fn, reads=(), writes=()):
        return self.add('pe', fn, reads, writes)

    def act(self, fn, reads=(), writes=()):
        return self.add('act', fn, reads, writes)

    def dve(self, fn, reads=(), writes=()):
        return self.add('dve', fn, reads, writes)

    def pool(self, fn, reads=(), writes=()):
        return self.add('pool', fn, reads, writes)

    def dma(self, eng, fn, reads=(), writes=()):
        return self.add(eng, fn, reads, writes, dma=True)

    def _needs_wait(self, op, d):
        if d.is_dma:
            return True
        if d.eng == op.eng:
            if op.eng == 'pe' and not op.is_dma:
                return False
            return (op.idx - d.idx) <= 2
        return True

    def emit(self, nc, stack):
        engsem = {e: stack.enter_context(nc.semaphore("ms_" + e)) for e in ENGS}
        dmasem = {}
        for e in ENGS:
            if self.dma_count[e]:
                dmasem[e] = [stack.enter_context(nc.semaphore("dq_%s_%d" % (e, i)))
                             for i in range(min(self.R, self.dma_count[e]))]
        for e in ENGS:
            for op in self.ops[e]:
                for d in op.deps:
                    if not d.is_dma and self._needs_wait(op, d):
                        d.needs_inc = True
        for e in ENGS:
            cnt = 0
            for op in self.ops[e]:
                if op.is_dma:
                    ring = dmasem[e]
                    op.dma_sem = ring[op.dma_n % len(ring)]
                    op.dma_val = 16 * (op.dma_n // len(ring) + 1)
                elif op.needs_inc:
                    cnt += 1
                    op.ms = cnt
        final_dma = []
        for e in ENGS:
            n = self.dma_count[e]
            if n:
                ring = dmasem[e]
                for i, s in enumerate(ring):
                    cnt_i = (n - i + len(ring) - 1) // len(ring)
                    if cnt_i > 0:
                        final_dma.append((s, 16 * cnt_i))
        stats = {}

        def emit_engine(e, h):
            seen = {}
            nw = 0
            for op in self.ops[e]:
                needed = {}
                for d in op.deps:
                    if not self._needs_wait(op, d):
                        continue
                    if d.is_dma:
                        sem, val = d.dma_sem, d.dma_val
                    else:
                        sem, val = engsem[d.eng], d.ms
                    k = id(sem)
                    if needed.get(k, (None, 0))[1] < val:
                        needed[k] = (sem, val)
                if op.is_dma and op.dma_val > 16:
                    k = id(op.dma_sem)
                    if needed.get(k, (None, 0))[1] < op.dma_val - 16:
                        needed[k] = (op.dma_sem, op.dma_val - 16)
                for k, (sem, val) in needed.items():
                    if seen.get(k, 0) < val:
                        h.wait_ge(sem, val)
                        seen[k] = val
                        nw += 1
                inst = op.fn(h)
                if op.is_dma:
                    inst.then_inc(op.dma_sem, 16)
                elif op.needs_inc:
                    inst.then_inc(engsem[e], 1)
            if e == 'sp':
                for s, v in final_dma:
                    h.wait_ge(s, v)
            stats[e] = (len(self.ops[e]), nw)

        with nc.Block() as block:
            for e in ENGS:
                getattr(block, BLOCK_NAME[e])(lambda h, e=e: emit_engine(e, h))
        return stats


class RR:
    def __init__(self, items):
        self.items = list(items)
        self.i = 0

    def next(self):
        v = self.items[self.i % len(self.items)]
        self.i += 1
        return v


def build_nc(L=DEPTH, NSEQ=2, do_ffn1=True, do_mix=True, do_ffn2=True, do_final=True,
             do_rec=True, do_attn=True, debug=False, no_outproj=False):
    nc = bass.Bass("TRN2", target_bir_lowering=False)
    NV = NVL * L + 8
    xT_d = nc.dram_tensor("xT", [NSEQ, D, T], F32, kind="ExternalInput").ap()
    wu_d = nc.dram_tensor("wu", [L * UPL, 128, 1024], F32, kind="ExternalInput").ap()
    vecs_d = nc.dram_tensor("vecs", [128, NV], F32, kind="ExternalInput").ap()
    cs_d = nc.dram_tensor("cs", [2, 128, T], F32, kind="ExternalInput").ap()
    mask_d = nc.dram_tensor("maskT", [128, MASKW], F32, kind="ExternalInput").ap()
    pm_d = nc.dram_tensor("pm", [128, 128], F32, kind="ExternalInput").ap()
    out_d = nc.dram_tensor("outT", [NSEQ, D, T], F32, kind="ExternalOutput").ap()

    S = Sched()
    with ExitStack() as st:
        def sb(name, shape, dt):
            return st.enter_context(nc.sbuf_tensor(name, shape, dt))

        xT = sb("xT_sb", [128, KC, T], F32)
        hT = sb("hT_sb", [128, KC, T], BF16)
        mg = sb("mg_sb", [128, 4, T], BF16)
        cosT = sb("cosT", [128, T], F32)
        sinT = sb("sinT", [128, T], F32)
        maskT = sb("maskT_sb", [128, MASKW], BF16)
        vecs = sb("vecs_sb", [128, NV], F32)
        dvec = sb("dvec_sb", [128, L * 16], F32)
        ones = sb("ones_sb", [128, 128], BF16)
        pmat = sb("pmat_sb", [128, 128], BF16)
        hlast = sb("hlast_sb", [128, 2], F32)
        dummy = sb("dummy_sb", [128, 2], F32)
        wring = sb("wring_sb", [128, NU, 1024], BF16)
        SCRW = 11800
        scr = sb("scr_sb", [128, SCRW], F32)
        sp_t = [st.enter_context(nc.psum_tensor("spair%d" % i, [128, 1024], F32)) for i in range(3)]
        b67 = [st.enter_context(nc.psum_tensor("bank%d" % i, [128, 512], F32)) for i in (6, 7)]
        banks = []
        for i in range(3):
            banks.append(sp_t[i][:, 0:512])
            banks.append(sp_t[i][:, 512:1024])
        banks.append(b67[0][:])
        banks.append(b67[1][:])
        P0, P1, P2, P3 = RR([0, 1]), RR([2, 3]), RR([4, 5]), RR([6, 7])
        PY = RR([4, 5, 6, 7])
        PS4 = RR([0, 1, 2, 3])
        PS6 = RR([0, 1, 2, 3, 4, 5])
        SPR = RR([0, 1, 2])
        P45 = RR([4, 5])

        def f32v(off, n):
            return scr[:, off:off + n]

        def bf16v(off, n):
            return scr[:, off:off + n // 2].bitcast(BF16)

        SQ = [bf16v(0, 512), bf16v(256, 512)]
        LNV = f32v(512, 512)
        RSTD = [f32v(1024, 512), f32v(1536, 512)]
        CB = 2048
        SG = [f32v(CB + 0, 512), f32v(CB + 512, 512)]
        AB = [bf16v(CB + 1024, GS * 512), bf16v(CB + 1024 + GS * 256, GS * 512)]
        o = CB
        XB = []
        for b in range(2):
            XB.append(f32v(o, 1028)); o += 1028
        XR = []
        for b in range(2):
            XR.append(f32v(o, 1024)); o += 1024
        XR16 = []
        for b in range(2):
            XR16.append(bf16v(o, 1024)); o += 512
        RB = f32v(o, 1024); o += 1024
        IB = f32v(o, 1024); o += 1024
        ABUF = f32v(o, 1024); o += 1024
        GL = []
        for b in range(3):
            GL.append(bf16v(o, 1024)); o += 512
        assert o <= SCRW, o
        o = CB
        QT = []
        KT = []
        for b in range(1):
            QT.append(bf16v(o, T)); o += T // 2
            KT.append([bf16v(o, T), bf16v(o + T // 2, T)]); o += T
        VA = []
        for b in range(1):
            VA.append(bf16v(o, 16 * 2 * 128)); o += 16 * 2 * 64
        NET, NPT = 3, 3
        ET = []
        for b in range(NET):
            ET.append(bf16v(o, 1024)); o += 512
        PT = []
        for b in range(NPT):
            PT.append(bf16v(o, 1024)); o += 512
        T1 = f32v(o, 512); o += 512
        T2 = f32v(o, 512); o += 512
        LND = T1
        RCP = T2
        Q16 = []
        for b in range(2):
            Q16.append(bf16v(o, 512)); o += 256
        assert o <= SCRW, o

        fence_n = [0]

        def fence():
            fence_n[0] += 1
            S.dve(lambda h: h.memset(dummy[:, 0:1], 0.0), reads=['dummyw'], writes=['scr', 'dummyw'])

        class WStream:
            def __init__(self):
                self.loaded = 0
                self.cur = 0
                self.total = NSEQ * L * UPL

            def ensure(self, upto):
                upto = min(upto, self.total)
                while self.loaded < upto:
                    m = self.loaded
                    slot = m % NU
                    src = wu_d[m % (L * UPL)]
                    S.dma('pool', lambda h, slot=slot, src=src: h.dma_start(out=wring[:, slot, :], in_=src),
                          writes=[('w', slot)])
                    self.loaded += 1

            def take(self, n):
                a = self.cur
                self.ensure(a + n)
                res = []
                for m in range(a, a + n):
                    slot = m % NU
                    res.append((wring[:, slot, :], ('w', slot)))
                self.cur += n
                return res

            def skip(self, n):
                raise NotImplementedError

        W = WStream()

        S.dma('sp', lambda h: h.dma_start(out=vecs[:], in_=vecs_d), writes=['vecs'])
        S.dma('sp', lambda h: h.dma_start(out=cosT[:], in_=cs_d[0]), writes=['cos'])
        S.dma('sp', lambda h: h.dma_start(out=sinT[:], in_=cs_d[1]), writes=['sin'])
        S.dma('pool', lambda h: h.dma_start(out=maskT[:, 0:1216], in_=mask_d[:, 0:1216]), writes=['mask'])
        S.dma('pool', lambda h: h.dma_start(out=maskT[:, 1216:MASKW], in_=mask_d[:, 1216:MASKW]), writes=['mask'])
        S.dve(lambda h: h.memset(ones[:], 1.0), writes=['ones'])
        S.dma('pool', lambda h: h.dma_start(out=pmat[:], in_=pm_d), writes=['pmat'])
        S.dve(lambda h: h.memset(dummy[:], 0.0), writes=['dummyw'])
        S.dve(lambda h: h.memset(hlast[:], 0.0), writes=['hlast'])
        for l in range(L):
            lam = vecs[:, l * NVL + 52: l * NVL + 56]
            S.act(lambda h, l=l, lam=lam: h.activation(out=dvec[:, l * 16:l * 16 + 4], in_=lam, func=AF.Exp, scale=-1.0),
                  reads=['vecs'], writes=[('dv', l)])
            S.act(lambda h, l=l: h.activation(out=dvec[:, l * 16:l * 16 + 4], in_=dvec[:, l * 16:l * 16 + 4], func=AF.Ln, bias=1.0),
                  reads=[('dv', l)], writes=[('dv', l)])
            S.dve(lambda h, l=l: h.tensor_scalar(out=dvec[:, l * 16 + 12:l * 16 + 16], in0=dvec[:, l * 16:l * 16 + 4],
                                                 scalar1=-8.0, scalar2=None, op0=ALU.mult),
                  reads=[('dv', l)], writes=[('dvk', l)])
            S.dve(lambda h, l=l: h.tensor_scalar(out=dvec[:, l * 16:l * 16 + 4], in0=dvec[:, l * 16:l * 16 + 4],
                                                 scalar1=-4.0, scalar2=None, op0=ALU.mult),
                  reads=[('dv', l), ('dvk', l)], writes=[('dv', l)])
            S.dve(lambda h, l=l: h.tensor_scalar(out=dvec[:, l * 16 + 4:l * 16 + 12], in0=vecs[:, l * NVL + 44:l * NVL + 52],
                                                 scalar1=0.5, scalar2=None, op0=ALU.mult),
                  reads=['vecs', ('dv', l)], writes=[('dv', l)])

        def tsl(tt):
            return slice(tt * TT, (tt + 1) * TT)

        def rstd_of(chunk_aps, chunk_keys, nfeat, sq_rr, rstd_buf, rstd_key, stat_pool, scr_dep=True):
            bank = stat_pool.next()
            n = len(chunk_aps)
            for i, (ap, key) in enumerate(zip(chunk_aps, chunk_keys)):
                sqi = sq_rr.next()
                S.act(lambda h, ap=ap, sqi=sqi: h.activation(out=SQ[sqi], in_=ap, func=AF.Square),
                      reads=[key, 'scr'], writes=[('sq', sqi)])
                S.pe(lambda h, sqi=sqi, bank=bank, i=i, n=n: h.matmul(banks[bank][:], lhsT=ones[:], rhs=SQ[sqi],
                                                                         start=(i == 0), stop=(i == n - 1)),
                     reads=[('sq', sqi), 'ones', 'scr'], writes=[('ps', bank)])
            S.act(lambda h, bank=bank: h.activation(out=LNV, in_=banks[bank][:], func=AF.Ln,
                                                    scale=1.0 / nfeat, bias=EPS),
                  reads=[('ps', bank), 'scr'], writes=['lnv'])
            S.act(lambda h: h.activation(out=rstd_buf, in_=LNV, func=AF.Exp, scale=-0.5),
                  reads=['lnv', 'scr'], writes=[rstd_key])

        SQRR = RR([0, 1])
        RSRR = RR([0, 1])

        def norm_tile(gcol, tt):
            ri = RSRR.next()
            rstd_of([xT[:, c, tsl(tt)] for c in range(KC)], [('x', c, tt) for c in range(KC)],
                    D, SQRR, RSTD[ri], ('rstd', ri), P3)
            for c in range(KC):
                S.dve(lambda h, c=c, tt=tt, ri=ri: h.scalar_tensor_tensor(
                    out=hT[:, c, tsl(tt)], in0=xT[:, c, tsl(tt)], scalar=vecs[:, gcol + c:gcol + c + 1],
                    in1=RSTD[ri], op0=ALU.mult, op1=ALU.mult),
                    reads=[('x', c, tt), ('rstd', ri), 'vecs', 'scr'], writes=[('h', c, tt)])

        normed = {'g': None}

        def norm_to_hT(gcol):
            if normed['g'] == gcol:
                return
            for tt in range(NTT):
                norm_tile(gcol, tt)
            normed['g'] = gcol

        def make_post(gnext):
            if gnext is None:
                return None

            def post(tt):
                norm_tile(gnext, tt)
                if tt == NTT - 1:
                    normed['g'] = gnext
            return post

        def ffn(gcol, post_tile=None):
            norm_to_hT(gcol)
            normed['g'] = None
            groups = [list(range(j, min(j + GS, NFC))) for j in range(0, NFC, GS)]
            its = []
            for gi, g in enumerate(groups):
                for tt in range(NTT):
                    its.append((gi, g, tt))
            gunits = {}
            sgrr = RR([0, 1])
            abrr = RR([0, 1])
            state = {}

            def GU(n):
                gi, g, tt = its[n]
                if gi not in gunits:
                    gunits[gi] = W.take(3 * len(g))
                units = gunits[gi]
                ab = abrr.next()
                state[n] = ab
                for jj in range(len(g)):
                    (wg, kg), (wu_, ku), _ = units[3 * jj:3 * jj + 3]
                    bG = P0.next()
                    bU = P1.next()
                    for k in range(KC):
                        S.pe(lambda h, wg=wg, k=k, bG=bG, tt=tt: h.matmul(
                            banks[bG][:], lhsT=wg[:, k * 128:(k + 1) * 128], rhs=hT[:, k, tsl(tt)],
                            start=(k == 0), stop=(k == KC - 1)),
                            reads=[kg, ('h', k, tt)], writes=[('ps', bG)])
                    for k in range(KC):
                        S.pe(lambda h, wu_=wu_, k=k, bU=bU, tt=tt: h.matmul(
                            banks[bU][:], lhsT=wu_[:, k * 128:(k + 1) * 128], rhs=hT[:, k, tsl(tt)],
                            start=(k == 0), stop=(k == KC - 1)),
                            reads=[ku, ('h', k, tt)], writes=[('ps', bU)])
                    si = sgrr.next()
                    S.act(lambda h, si=si, bG=bG: h.activation(out=SG[si], in_=banks[bG][:], func=AF.Silu),
                          reads=[('ps', bG), 'scr'], writes=[('sg', si)])
                    S.dve(lambda h, si=si, bU=bU, ab=ab, jj=jj: h.tensor_tensor(
                        out=AB[ab][:, jj * 512:(jj + 1) * 512], in0=banks[bU][:], in1=SG[si], op=ALU.mult),
                        reads=[('ps', bU), ('sg', si), 'scr'], writes=[('ab', ab, jj)])

            def Y(n):
                gi, g, tt = its[n]
                units = gunits[gi]
                ab = state.pop(n)
                for dc in range(KC):
                    bY = PY.next()
                    for jj in range(len(g)):
                        wo, ko = units[3 * jj + 2]
                        S.pe(lambda h, wo=wo, dc=dc, bY=bY, ab=ab, jj=jj, ng=len(g): h.matmul(
                            banks[bY][:], lhsT=wo[:, dc * 128:(dc + 1) * 128], rhs=AB[ab][:, jj * 512:(jj + 1) * 512],
                            start=(jj == 0), stop=(jj == ng - 1)),
                            reads=[ko, ('ab', ab, jj), 'scr'], writes=[('ps', bY)])
                    S.dve(lambda h, dc=dc, bY=bY, tt=tt: h.scalar_tensor_tensor(
                        out=xT[:, dc, tsl(tt)], in0=banks[bY][:], scalar=0.5, in1=xT[:, dc, tsl(tt)],
                        op0=ALU.mult, op1=ALU.add),
                        reads=[('ps', bY), ('x', dc, tt)], writes=[('x', dc, tt)])

            GU(0)
            for n in range(len(its)):
                if n + 1 < len(its):
                    GU(n + 1)
                Y(n)
                if post_tile is not None and its[n][0] == len(groups) - 1 and its[n][2] >= 1:
                    post_tile(its[n][2] - 1)
            if post_tile is not None:
                post_tile(NTT - 1)

        def out_proj(gcol, nfeat_units, post_tile=None):
            for tt in range(NTT):
                ri = RSRR.next()
                rstd_of([mg[:, c, tsl(tt)] for c in range(4)], [('mg', c, tt) for c in range(4)],
                        512, SQRR, RSTD[ri], ('rstd', ri), P0)
                for c in range(4):
                    S.dve(lambda h, c=c, tt=tt, ri=ri: h.scalar_tensor_tensor(
                        out=mg[:, c, tsl(tt)], in0=mg[:, c, tsl(tt)], scalar=vecs[:, gcol + c:gcol + c + 1],
                        in1=RSTD[ri], op0=ALU.mult, op1=ALU.mult),
                        reads=[('mg', c, tt), ('rstd', ri), 'vecs', 'scr'], writes=[('mg', c, tt)])
                for dc in range(KC):
                    bY = P2.next()
                    for c in range(4):
                        wo, ko = nfeat_units[c]
                        S.pe(lambda h, wo=wo, dc=dc, bY=bY, c=c, tt=tt: h.matmul(
                            banks[bY][:], lhsT=wo[:, dc * 128:(dc + 1) * 128], rhs=mg[:, c, tsl(tt)],
                            start=(c == 0), stop=(c == 3)),
                            reads=[ko, ('mg', c, tt)], writes=[('ps', bY)])
                    S.dve(lambda h, dc=dc, bY=bY, tt=tt: h.tensor_tensor(
                        out=xT[:, dc, tsl(tt)], in0=banks[bY][:], in1=xT[:, dc, tsl(tt)], op=ALU.add),
                        reads=[('ps', bY), ('x', dc, tt)], writes=[('x', dc, tt)])
                if post_tile is not None and tt >= 1:
                    post_tile(tt - 1)
            if post_tile is not None:
                post_tile(NTT - 1)

        def rec_branch(l):
            vb = l * NVL
            units_bd = W.take(1)
            wbd, kbd = units_bd[0]
            pieces = [(rc, hf) for rc in range(4) for hf in range(2)]
            runits = {}

            def vec_aps(rc):
                cw = [vecs[:, vb + 24 + j * 4 + rc: vb + 24 + j * 4 + rc + 1] for j in range(4)]
                cb = vecs[:, vb + 40 + rc: vb + 41 + rc]
                ba = dvec[:, l * 16 + 4 + rc: l * 16 + 5 + rc]
                bx = dvec[:, l * 16 + 8 + rc: l * 16 + 9 + rc]
                kl = dvec[:, l * 16 + rc: l * 16 + rc + 1]
                kl2 = dvec[:, l * 16 + 12 + rc: l * 16 + 13 + rc]
                return cw, cb, ba, bx, kl, kl2

            def stA(p):
                rc, hf = pieces[p]
                if p == 0:
                    runits[0] = W.take(2)
                    runits[1] = W.take(2)
                elif hf == 0 and rc + 1 < 4:
                    runits[rc + 1] = W.take(2)
                (wxb, kxb), (wgb, kgb) = runits[rc]
                xi, gi = p % 2, p % 3
                xb, gl = XB[xi], GL[gi]
                if hf == 0:
                    S.dve(lambda h, xb=xb: h.memset(xb[:, 0:3], 0.0), reads=['scr'], writes=[('xb', xi)])
                else:
                    xprev = XB[(p - 1) % 2]
                    S.dve(lambda h, xb=xb, xprev=xprev: h.tensor_copy(out=xb[:, 0:3], in_=xprev[:, 1024:1027]),
                          reads=[('xb', (p - 1) % 2), 'scr'], writes=[('xb', xi)])
                for t2 in range(2):
                    tt = hf * 2 + t2
                    b0 = PS4.next()
                    for k in range(KC):
                        S.pe(lambda h, k=k, b0=b0, tt=tt, wxb=wxb: h.matmul(
                            banks[b0][:], lhsT=wxb[:, k * 128:(k + 1) * 128], rhs=hT[:, k, tsl(tt)],
                            start=(k == 0), stop=(k == KC - 1)),
                            reads=[kxb, ('h', k, tt)], writes=[('ps', b0)])
                    S.act(lambda h, b0=b0, t2=t2, xb=xb: h.activation(out=xb[:, 3 + t2 * 512:3 + (t2 + 1) * 512],
                                                                      in_=banks[b0][:], func=AF.Copy),
                          reads=[('ps', b0), 'scr'], writes=[('xb', xi)])
                    b1 = PS4.next()
                    for k in range(KC):
                        S.pe(lambda h, k=k, b1=b1, tt=tt, wgb=wgb: h.matmul(
                            banks[b1][:], lhsT=wgb[:, k * 128:(k + 1) * 128], rhs=hT[:, k, tsl(tt)],
                            start=(k == 0), stop=(k == KC - 1)),
                            reads=[kgb, ('h', k, tt)], writes=[('ps', b1)])
                    S.act(lambda h, b1=b1, t2=t2, gl=gl: h.activation(out=gl[:, t2 * 512:(t2 + 1) * 512],
                                                                      in_=banks[b1][:], func=AF.Gelu_apprx_tanh),
                          reads=[('ps', b1), 'scr'], writes=[('gl', gi)])

            def stB(p):
                rc, hf = pieces[p]
                cw, cb, ba, bx, kl, kl2 = vec_aps(rc)
                xi = p % 2
                xb, xr, xr16 = XB[xi], XR[xi], XR16[xi]
                S.pool(lambda h, cw=cw, cb=cb, xb=xb, xr=xr: h.tensor_scalar(
                    out=xr, in0=xb[:, 3:1027], scalar1=cw[3], scalar2=cb, op0=ALU.mult, op1=ALU.add),
                    reads=[('xb', xi), 'vecs', 'scr'], writes=[('xr', xi)])
                for j in (2, 1, 0):
                    S.dve(lambda h, cw=cw, j=j, xb=xb, xr=xr: h.scalar_tensor_tensor(
                        out=xr, in0=xb[:, j:j + 1024], scalar=cw[j], in1=xr, op0=ALU.mult, op1=ALU.add),
                        reads=[('xb', xi), ('xr', xi), 'vecs', 'scr'], writes=[('xr', xi)])
                S.act(lambda h, xr=xr, xr16=xr16: h.activation(out=xr16, in_=xr, func=AF.Copy),
                      reads=[('xr', xi), 'scr'], writes=[('xr16', xi)])

            def stC(p):
                rc, hf = pieces[p]
                cw, cb, ba, bx, kl, kl2 = vec_aps(rc)
                xi = p % 2
                xr16 = XR16[xi]
                for t2 in range(2):
                    ba_ = P45.next()
                    S.pe(lambda h, ba_=ba_, t2=t2, rc=rc, xr16=xr16: h.matmul(
                        banks[ba_][:], lhsT=wbd[:, rc * 128:(rc + 1) * 128], rhs=xr16[:, t2 * 512:(t2 + 1) * 512],
                        start=True, stop=True), reads=[kbd, ('xr16', xi), 'scr'], writes=[('ps', ba_)])
                    S.act(lambda h, ba_=ba_, t2=t2, ba=ba: h.activation(
                        out=RB[:, t2 * 512:(t2 + 1) * 512], in_=banks[ba_][:], func=AF.Tanh, bias=ba, scale=0.5),
                        reads=[('ps', ba_), ('dv', l), 'scr'], writes=['rb'])
                    bx_ = P45.next()
                    S.pe(lambda h, bx_=bx_, t2=t2, rc=rc, xr16=xr16: h.matmul(
                        banks[bx_][:], lhsT=wbd[:, 512 + rc * 128:512 + (rc + 1) * 128],
                        rhs=xr16[:, t2 * 512:(t2 + 1) * 512], start=True, stop=True),
                        reads=[kbd, ('xr16', xi), 'scr'], writes=[('ps', bx_)])
                    S.act(lambda h, bx_=bx_, t2=t2, bx=bx: h.activation(
                        out=IB[:, t2 * 512:(t2 + 1) * 512], in_=banks[bx_][:], func=AF.Tanh, bias=bx, scale=0.5),
                        reads=[('ps', bx_), ('dv', l), 'scr'], writes=['ib'])
                S.act(lambda h, kl=kl: h.activation(out=ABUF, in_=RB, func=AF.Exp, scale=kl, bias=kl),
                      reads=['rb', ('dv', l), 'scr'], writes=['abuf'])
                S.act(lambda h, kl2=kl2: h.activation(out=RB, in_=RB, func=AF.Exp, scale=kl2, bias=kl2),
                      reads=['rb', ('dvk', l), 'scr'], writes=['rb'])
                S.act(lambda h: h.activation(out=RB, in_=RB, func=AF.Sqrt, scale=-1.0, bias=1.0),
                      reads=['rb', 'scr'], writes=['rb'])

            def stD(p):
                rc, hf = pieces[p]
                xi, gi = p % 2, p % 3
                xr, gl = XR[xi], GL[gi]
                S.dve(lambda h: h.scalar_tensor_tensor(out=IB, in0=IB, scalar=1.0, in1=RB, op0=ALU.add, op1=ALU.mult),
                      reads=['ib', 'rb', 'scr'], writes=['ib'])
                S.dve(lambda h, xr=xr: h.scalar_tensor_tensor(out=IB, in0=IB, scalar=0.5, in1=xr, op0=ALU.mult, op1=ALU.mult),
                      reads=['ib', ('xr', xi), 'scr'], writes=['ib'])
                if hf == 0:
                    S.dve(lambda h, xr=xr: h.tensor_tensor_scan(out=xr, data0=ABUF, data1=IB, initial=0.0,
                                                                op0=ALU.mult, op1=ALU.add),
                          reads=['abuf', 'ib', ('xr', xi), 'scr'], writes=[('xr', xi)])
                    S.dve(lambda h, xr=xr: h.tensor_copy(out=hlast[:, 0:1], in_=xr[:, 1023:1024]),
                          reads=[('xr', xi), 'scr'], writes=['hlast'])
                else:
                    S.dve(lambda h, xr=xr: h.tensor_tensor_scan(out=xr, data0=ABUF, data1=IB, initial=hlast[:, 0:1],
                                                                op0=ALU.mult, op1=ALU.add),
                          reads=['abuf', 'ib', ('xr', xi), 'hlast', 'scr'], writes=[('xr', xi)])
                S.pool(lambda h, rc=rc, hf=hf, xr=xr, gl=gl: h.tensor_tensor(
                    out=mg[:, rc, hf * 1024:(hf + 1) * 1024], in0=xr, in1=gl, op=ALU.mult),
                    reads=[('xr', xi), ('gl', gi), 'scr'], writes=[('mg', rc, 2 * hf), ('mg', rc, 2 * hf + 1)])

            NP = len(pieces)
            stA(0)
            stA(1)
            stB(0)
            for p in range(NP):
                stC(p)
                if p + 2 < NP:
                    stA(p + 2)
                if p + 1 < NP:
                    stB(p + 1)
                stD(p)
            wo_units = W.take(4)
            out_proj(l * NVL + 60, wo_units)

        def attn_branch(l, post_tile=None):
            vb = l * NVL
            qkrr = RR([0])
            varr = RR([0])
            etrr = RR(list(range(NET)))
            ptrr = RR(list(range(NPT)))
            for hp in range(4):
                (wv, kv), (wq, kq), (wk, kk_) = W.take(3)
                qb = qkrr.next()
                vbuf = varr.next()
                VAv = VA[vbuf].rearrange("p (kb e f) -> p kb e f", kb=16, e=2)
                for kq4 in range(4):
                    bV = PS6.next()
                    for q4 in range(4):
                        kb = kq4 * 4 + q4
                        for k in range(KC):
                            S.pe(lambda h, k=k, kb=kb, q4=q4, bV=bV, wv=wv: h.matmul(
                                banks[bV][:, q4 * 128:(q4 + 1) * 128], lhsT=hT[:, k, kb * 128:(kb + 1) * 128],
                                rhs=wv[:, k * 128:(k + 1) * 128], start=(k == 0), stop=(k == KC - 1)),
                                reads=[kv, ('h', k, kb // 4)], writes=[('ps', bV)])
                    bview = banks[bV][:].rearrange("p (q e f) -> p q e f", q=4, e=2)
                    S.act(lambda h, bview=bview, VAv=VAv, kq4=kq4: h.activation(
                        out=VAv[:, kq4 * 4:(kq4 + 1) * 4, 0, 0:64], in_=bview[:, :, 0, :], func=AF.Copy),
                        reads=[('ps', bV), 'scr'], writes=[('va', vbuf, kq4, 0)])
                    S.act(lambda h, bview=bview, VAv=VAv, kq4=kq4: h.activation(
                        out=VAv[:, kq4 * 4:(kq4 + 1) * 4, 1, 64:128], in_=bview[:, :, 1, :], func=AF.Copy),
                        reads=[('ps', bV), 'scr'], writes=[('va', vbuf, kq4, 1)])
                tiles = []
                for (wa, ka, dst, dkey) in ((wq, kq, QT[qb], 'qt'), (wk, kk_, KT[qb], 'kt')):
                    for tt in range(NTT):
                        tiles.append((wa, ka, dst, dkey, tt))
                tstate = {}

                def RP(i):
                    wa, ka, dst, dkey, tt = tiles[i]
                    bA = PS6.next()
                    qi = i % 2
                    for k in range(KC):
                        S.pe(lambda h, k=k, bA=bA, tt=tt, wa=wa: h.matmul(
                            banks[bA][:], lhsT=wa[:, k * 128:(k + 1) * 128], rhs=hT[:, k, tsl(tt)],
                            start=(k == 0), stop=(k == KC - 1)),
                            reads=[ka, ('h', k, tt)], writes=[('ps', bA)])
                    S.act(lambda h, bA=bA, qi=qi: h.activation(out=Q16[qi], in_=banks[bA][:], func=AF.Copy),
                          reads=[('ps', bA), 'scr'], writes=[('q16', qi)])
                    tstate[i] = bA

                def RR_(i):
                    wa, ka, dst, dkey, tt = tiles[i]
                    bA = tstate.pop(i)
                    qi = i % 2
                    bB = PS6.next()
                    S.pe(lambda h, bB=bB, qi=qi: h.matmul(banks[bB][:], lhsT=pmat[:], rhs=Q16[qi], start=True, stop=True),
                         reads=['pmat', ('q16', qi), 'scr'], writes=[('ps', bB)])
                    S.dve(lambda h, bA=bA, tt=tt: h.tensor_tensor(out=T1, in0=banks[bA][:], in1=cosT[:, tsl(tt)], op=ALU.mult),
                          reads=[('ps', bA), 'cos', 'scr'], writes=['t1'])
                    S.dve(lambda h, bB=bB, tt=tt: h.tensor_tensor(out=T2, in0=banks[bB][:], in1=sinT[:, tsl(tt)], op=ALU.mult),
                          reads=[('ps', bB), 'sin', 'scr'], writes=['t2'])
                    if dkey == 'qt':
                        S.dve(lambda h, dst=dst, tt=tt: h.tensor_tensor(out=dst[:, tsl(tt)], in0=T1, in1=T2, op=ALU.add),
                              reads=['t1', 't2', 'scr'], writes=[(dkey, qb, tt)])
                    else:
                        for e in range(2):
                            rows = slice(e * 64, (e + 1) * 64)
                            S.dve(lambda h, dst=dst, tt=tt, e=e, rows=rows: h.tensor_tensor(
                                out=dst[e][rows, tsl(tt)], in0=T1[rows, :], in1=T2[rows, :], op=ALU.add),
                                reads=['t1', 't2', 'ktz', 'scr'], writes=[(dkey, qb, tt, e)])

                RP(0)
                for i in range(len(tiles)):
                    if i + 1 < len(tiles):
                        RP(i + 1)
                    RR_(i)
                pairs = []
                for e in range(2):
                    for g in range(NTT):
                        for kp in range((4 * g + 4) // 2):
                            pairs.append((e, g, 2 * kp))
                ust = {}

                def qlo_of(g, kb):
                    return 128 * max(0, kb - 4 * g)

                def SUP(n):
                    e, g, kb0 = pairs[n]
                    sp = SPR.next()
                    kbA, kbB = kb0 + 1, kb0
                    qa, qb_ = qlo_of(g, kbA), qlo_of(g, kbB)
                    wA, wB = 512 - qa, 512 - qb_
                    S.pe(lambda h, sp=sp, e=e, kb=kbA, g=g, qb=qb, qa=qa: h.matmul(
                        sp_t[sp][:, qa:512], lhsT=KT[qb][e][:, kb * 128:(kb + 1) * 128],
                        rhs=QT[qb][:, g * 512 + qa:(g + 1) * 512], start=True, stop=True),
                        reads=[('kt', qb, kbA // 4, e), 'ktz', ('qt', qb, g), 'scr'], writes=[('ps', 2 * sp)])
                    S.pe(lambda h, sp=sp, e=e, kb=kbB, g=g, qb=qb, qb_=qb_, wB=wB: h.matmul(
                        sp_t[sp][:, 512:512 + wB], lhsT=KT[qb][e][:, kb * 128:(kb + 1) * 128],
                        rhs=QT[qb][:, g * 512 + qb_:(g + 1) * 512], start=True, stop=True),
                        reads=[('kt', qb, kbB // 4, e), 'ktz', ('qt', qb, g), 'scr'], writes=[('ps', 2 * sp + 1)])
                    ei = etrr.next()
                    S.act(lambda h, sp=sp, ei=ei, qa=qa, wB=wB: h.activation(
                        out=ET[ei][:, qa:512 + wB], in_=sp_t[sp][:, qa:512 + wB], func=AF.Exp, scale=0.125),
                        reads=[('ps', 2 * sp), ('ps', 2 * sp + 1), 'scr'], writes=[('et', ei)])
                    pi = ptrr.next()
                    offA = 512 * g - 128 * kbA + 384
                    offB = 512 * g - 128 * kbB + 384
                    if qa == 0 and qb_ == 0:
                        mb = maskT[:, offA:offA + 512]
                        m_ap = bass.AP(mb.tensor, mb.offset, [list(mb.ap[0]), [128, 2], [1, 512]])
                        S.dve(lambda h, ei=ei, pi=pi, m_ap=m_ap: h.tensor_tensor(
                            out=PT[pi].rearrange("p (a b) -> p a b", a=2), in0=ET[ei].rearrange("p (a b) -> p a b", a=2),
                            in1=m_ap, op=ALU.mult),
                            reads=[('et', ei), 'mask', 'scr'], writes=[('pt', pi)])
                    else:
                        S.dve(lambda h, ei=ei, pi=pi, offA=offA, qa=qa: h.tensor_tensor(
                            out=PT[pi][:, qa:512], in0=ET[ei][:, qa:512], in1=maskT[:, offA + qa:offA + 512], op=ALU.mult),
                            reads=[('et', ei), 'mask', 'scr'], writes=[('pt', pi)])
                        S.dve(lambda h, ei=ei, pi=pi, offB=offB, qb_=qb_, wB=wB: h.tensor_tensor(
                            out=PT[pi][:, 512:512 + wB], in0=ET[ei][:, 512:512 + wB],
                            in1=maskT[:, offB + qb_:offB + 512], op=ALU.mult),
                            reads=[('et', ei), 'mask', 'scr'], writes=[('pt', pi)])
                    ust[n] = (pi, qa, qb_, wB)

                obank = {}

                def PVP(n):
                    e, g, kb0 = pairs[n]
                    pi, qa, qb_, wB = ust.pop(n)
                    first = (kb0 == 0)
                    last = (kb0 + 2 == 4 * g + 4)
                    if first:
                        obank[(e, g)] = P3.next()
                    bO = obank[(e, g)]
                    S.pe(lambda h, bO=bO, kb0=kb0, e=e, pi=pi, first=first, VAv=VAv, qa=qa: h.matmul(
                        banks[bO][:, qa:512], lhsT=VAv[:, kb0 + 1, e, :], rhs=PT[pi][:, qa:512], start=first, stop=False),
                        reads=[('va', vbuf, (kb0 + 1) // 4, 0), ('va', vbuf, (kb0 + 1) // 4, 1), 'vaones', ('pt', pi), 'scr'],
                        writes=[('ps', bO)])
                    S.pe(lambda h, bO=bO, kb0=kb0, e=e, pi=pi, last=last, VAv=VAv, qb_=qb_, wB=wB: h.matmul(
                        banks[bO][:, qb_:512], lhsT=VAv[:, kb0, e, :], rhs=PT[pi][:, 512:512 + wB], start=False, stop=last),
                        reads=[('va', vbuf, kb0 // 4, 0), ('va', vbuf, kb0 // 4, 1), 'vaones', ('pt', pi), 'scr'],
                        writes=[('ps', bO)])
                    if last:
                        nrows = slice(e * 64, (e + 1) * 64)
                        drows = slice((1 - e) * 64, (2 - e) * 64)
                        S.act(lambda h, bO=bO, drows=drows: h.activation(out=LND[drows, :], in_=banks[bO][drows, :], func=AF.Ln),
                              reads=[('ps', bO), 'scr'], writes=['t1'])
                        S.act(lambda h, drows=drows: h.activation(out=RCP[drows, :], in_=LND[drows, :], func=AF.Exp, scale=-1.0),
                              reads=['t1', 'scr'], writes=['t2'])
                        S.dve(lambda h, bO=bO, nrows=nrows, drows=drows, g=g, hp=hp: h.tensor_tensor(
                            out=mg[nrows, hp, tsl(g)], in0=banks[bO][nrows, :], in1=RCP[drows, :], op=ALU.mult),
                            reads=[('ps', bO), 't2', 'scr'], writes=[('mg', hp, g)])

                LOOK = 2
                for n in range(min(LOOK, len(pairs))):
                    SUP(n)
                for n in range(len(pairs)):
                    if n + LOOK < len(pairs):
                        SUP(n + LOOK)
                    PVP(n)
            wo_units = W.take(4)
            if not no_outproj:
                out_proj(l * NVL + 56, wo_units, post_tile)

        def set_va_ones():
            for b in range(1):
                VAv0 = VA[b].rearrange("p (kb e f) -> p kb e f", kb=16, e=2)
                S.dve(lambda h, VAv0=VAv0: h.memset(VAv0[:, :, 0, 64:128], 1.0), reads=['scr'], writes=['vaones'])
                S.dve(lambda h, VAv0=VAv0: h.memset(VAv0[:, :, 1, 0:64], 1.0), reads=['scr'], writes=['vaones'])
                S.dve(lambda h, b=b: h.memset(KT[b][0][64:128, :], 0.0), reads=['scr'], writes=['ktz'])
                S.dve(lambda h, b=b: h.memset(KT[b][1][0:64, :], 0.0), reads=['scr'], writes=['ktz'])

        for s in range(NSEQ):
            xsrc = xT_d[s].rearrange("(c p) t -> p c t", p=128)
            for tt in range(NTT):
                S.dma('sp', lambda h, tt=tt, xsrc=xsrc: h.dma_start(out=xT[:, :, tsl(tt)], in_=xsrc[:, :, tsl(tt)]),
                      writes=[('x', c, tt) for c in range(KC)])
            phases = []
            for l in range(L):
                vb = l * NVL
                phases.append(('ffn1', l, vb + 0, do_ffn1))
                phases.append(('mix', l, vb + 8, do_mix))
                phases.append(('ffn2', l, vb + 16, do_ffn2))
            normed['g'] = None
            for pi_, (kind, l, gcol, on) in enumerate(phases):
                gnext = None
                for (k2, l2, g2, on2) in phases[pi_ + 1:]:
                    if on2:
                        gnext = g2
                        break
                post = make_post(gnext)
                if kind in ('ffn1', 'ffn2'):
                    if on:
                        ffn(gcol, post)
                    else:
                        W.take(66)
                else:
                    if on:
                        fence()
                        norm_to_hT(gcol)
                        normed['g'] = None
                        if do_rec:
                            rec_branch(l)
                        else:
                            W.take(13)
                        fence()
                        if do_attn:
                            set_va_ones()
                            attn_branch(l, post)
                        else:
                            W.take(16)
                        fence()
                    else:
                        W.take(29)
            odst = out_d[s].rearrange("(c p) t -> p c t", p=128)
            gcol = NVL * L
            for tt in range(NTT):
                if do_final:
                    ri = RSRR.next()
                    rstd_of([xT[:, c, tsl(tt)] for c in range(KC)], [('x', c, tt) for c in range(KC)],
                            D, SQRR, RSTD[ri], ('rstd', ri), P3)
                    for c in range(KC):
                        S.dve(lambda h, c=c, tt=tt, ri=ri: h.scalar_tensor_tensor(
                            out=xT[:, c, tsl(tt)], in0=xT[:, c, tsl(tt)], scalar=vecs[:, gcol + c:gcol + c + 1],
                            in1=RSTD[ri], op0=ALU.mult, op1=ALU.mult),
                            reads=[('x', c, tt), ('rstd', ri), 'vecs', 'scr'], writes=[('x', c, tt)])
                S.dma('sp', lambda h, tt=tt, odst=odst: h.dma_start(out=odst[:, :, tsl(tt)], in_=xT[:, :, tsl(tt)]),
                      reads=[('x', c, tt) for c in range(KC)])
        if debug:
            dh = nc.dram_tensor("dbg_h", [128, KC * T], BF16, kind="ExternalOutput").ap()
            dm = nc.dram_tensor("dbg_mg", [128, 4 * T], BF16, kind="ExternalOutput").ap()
            ds = nc.dram_tensor("dbg_scr", [128, SCRW], F32, kind="ExternalOutput").ap()
            dw = nc.dram_tensor("dbg_w", [128, NU * 1024], BF16, kind="ExternalOutput").ap()
            S.dma('sp', lambda h: h.dma_start(out=dh, in_=hT[:].rearrange("p c t -> p (c t)")),
                  reads=[('h', c, tt) for c in range(KC) for tt in range(NTT)])
            S.dma('sp', lambda h: h.dma_start(out=dm, in_=mg[:].rearrange("p c t -> p (c t)")),
                  reads=[('mg', c, tt) for c in range(4) for tt in range(NTT)])
            S.dve(lambda h: h.memset(dummy[:, 1:2], 0.0), reads=['scr'], writes=['scrdump'])
            S.dma('sp', lambda h: h.dma_start(out=ds, in_=scr[:]), reads=['scrdump'])
            S.dma('sp', lambda h: h.dma_start(out=dw, in_=wring[:].rearrange("p u f -> p (u f)")),
                  reads=[('w', u) for u in range(NU)])
        stats = S.emit(nc, st)
    return nc, stats


def _colunit(Wm, cols):
    return np.ascontiguousarray(Wm[:, cols].reshape(8, 128, 128).transpose(1, 0, 2).reshape(128, 1024))


def make_units(inp, L):
    units = np.empty((L * UPL, 128, 1024), np.float32)
    ar = np.arange(128)
    u = 0
    for l in range(L):
        for pre in ("ffn1", "mix", "ffn2"):
            if pre != "mix":
                w_in = inp[pre + "_w_in"][l]
                w_out = inp[pre + "_w_out"][l]
                for j in range(NFC):
                    units[u] = _colunit(w_in, j * 128 + ar); u += 1
                    units[u] = _colunit(w_in, DFF + j * 128 + ar); u += 1
                    units[u] = w_out[j * 128:(j + 1) * 128, :]; u += 1
            else:
                w_in = inp["w_in"][l]
                w_out = inp["w_out"][l]
                bd = np.zeros((128, 1024), np.float32)
                for gi, nm in enumerate(("rg_w_a", "rg_w_x")):
                    wg = inp[nm][l]
                    for rc in range(4):
                        c0 = gi * 512 + rc * 128
                        bd[0:64, c0:c0 + 64] = wg[2 * rc]
                        bd[64:128, c0 + 64:c0 + 128] = wg[2 * rc + 1]
                units[u] = bd; u += 1
                for rc in range(4):
                    units[u] = _colunit(w_in, 1536 + rc * 128 + ar); u += 1
                    units[u] = _colunit(w_in, 2048 + rc * 128 + ar); u += 1
                for rc in range(4):
                    units[u] = w_out[512 + rc * 128:512 + (rc + 1) * 128, :]; u += 1
                for hp in range(4):
                    units[u] = _colunit(w_in, 1024 + hp * 128 + ar); u += 1
                    units[u] = _colunit(w_in, hp * 128 + ar); u += 1
                    units[u] = _colunit(w_in, 512 + hp * 128 + ar); u += 1
                for hp in range(4):
                    units[u] = w_out[hp * 128:(hp + 1) * 128, :]; u += 1
    assert u == L * UPL
    return units


def make_vecs(inp, L):
    NV = NVL * L + 8
    v = np.zeros((128, NV), np.float32)

    def cols(vec, n):
        return np.asarray(vec, np.float32).reshape(n, 128).T

    for l in range(L):
        b = l * NVL
        v[:, b + 0:b + 8] = cols(inp["ffn1_norm"][l], 8)
        v[:, b + 8:b + 16] = cols(inp["mix_norm"][l], 8)
        v[:, b + 16:b + 24] = cols(inp["ffn2_norm"][l], 8)
        for j in range(4):
            v[:, b + 24 + 4 * j:b + 28 + 4 * j] = cols(inp["conv_w"][l][j], 4)
        v[:, b + 40:b + 44] = cols(inp["conv_b"][l], 4)
        v[:, b + 44:b + 48] = cols(inp["rg_b_a"][l], 4)
        v[:, b + 48:b + 52] = cols(inp["rg_b_x"][l], 4)
        v[:, b + 52:b + 56] = cols(inp["rg_lambda"][l], 4)
        v[:, b + 56:b + 60] = cols(inp["attn_out_norm"][l], 4)
        v[:, b + 60:b + 64] = cols(inp["rec_out_norm"][l], 4)
    v[:, NVL * L:NVL * L + 8] = cols(inp["final_norm"], 8)
    return v


def make_tables():
    pos = np.arange(T, dtype=np.float32)
    inv = (np.float32(10000.0) ** (-np.arange(0, 64, 2, dtype=np.float32) / np.float32(64))).astype(np.float32)
    ang = (pos[:, None] * inv[None, :]).astype(np.float32)
    cos = np.cos(ang).astype(np.float32).T
    sin = np.sin(ang).astype(np.float32).T
    p = np.arange(128)
    dd = p % 64
    f = dd % 32
    cs = np.empty((2, 128, T), np.float32)
    cs[0] = cos[f]
    cs[1] = np.where((dd < 32)[:, None], -sin[f], sin[f])
    xx = np.arange(MASKW)[None, :]
    jj = np.arange(128)[:, None]
    dist = xx - jj - 384
    c = ((dist >= 0) & (dist <= 128)).astype(np.float32)
    c += ((dist >= 0) & (dist % 4 == 0) & (dist <= 512)).astype(np.float32)
    c += ((dist >= 0) & (dist % 16 == 0) & (dist <= 2048)).astype(np.float32)
    ar = np.arange(128)
    d = ar % 64
    perm = (ar // 64) * 64 + np.where(d < 32, d + 32, d - 32)
    pm = np.zeros((128, 128), np.float32)
    pm[perm, ar] = 1.0
    return cs, c.astype(np.float32), pm


_CACHE = {}


def kernel(**inputs):
    inp = {k: np.asarray(v) for k, v in inputs.items()}
    x = inp["x"].astype(np.float32, copy=False)
    B = x.shape[0]
    n_cores = 8
    nseq = B // n_cores
    L = inp["ffn1_norm"].shape[0]
    key = (L, nseq)
    if key not in _CACHE:
        _CACHE[key] = build_nc(L=L, NSEQ=nseq)[0]
    nc = _CACHE[key]
    units = make_units(inp, L)
    vecs = make_vecs(inp, L)
    cs, maskT, pm = make_tables()
    in_maps = []
    for c in range(n_cores):
        xs = np.ascontiguousarray(x[c * nseq:(c + 1) * nseq].transpose(0, 2, 1))
        in_maps.append({"xT": xs, "wu": units, "vecs": vecs, "cs": cs, "maskT": maskT, "pm": pm})
    res = run_bass_kernel_spmd(nc, in_maps, core_ids=list(range(n_cores)))
    outs = [np.asarray(r["outT"]).transpose(0, 2, 1) for r in res.results]
    return np.ascontiguousarray(np.concatenate(outs, axis=0)).astype(np.float32, copy=False)
```

```python
import numpy as np
from contextlib import ExitStack
import concourse.bass as bass
import concourse.mybir as mybir
from concourse.bass_utils import run_bass_kernel_spmd

F32 = mybir.dt.float32
BF16 = mybir.dt.bfloat16
AF = mybir.ActivationFunctionType
ALU = mybir.AluOpType

D = 1024
T = 2048
TT = 512
NTT = 4
KC = 8
DFF = 2816
NFC = 22
DEPTH = 4
EPS = 1e-6
UPL = 161
NVL = 64
MASKW = 2432
GS = 2
NU = 12

ENGS = ['pe', 'act', 'dve', 'pool', 'sp']
BLOCK_NAME = {'pe': 'tensor', 'act': 'scalar', 'dve': 'vector', 'pool': 'gpsimd', 'sp': 'sync'}


class Op:
    __slots__ = ('eng', 'fn', 'idx', 'is_dma', 'deps', 'ms', 'needs_inc', 'dma_n', 'dma_sem', 'dma_val')


class Sched:
    def __init__(self, dma_ring=8):
        self.ops = {e: [] for e in ENGS}
        self.last_w = {}
        self.readers = {}
        self.dma_count = {e: 0 for e in ENGS}
        self.R = dma_ring

    def add(self, eng, fn, reads=(), writes=(), dma=False):
        op = Op()
        op.eng = eng
        op.fn = fn
        op.idx = len(self.ops[eng])
        op.is_dma = dma
        op.needs_inc = False
        op.ms = 0
        deps = {}
        for r in reads:
            w = self.last_w.get(r)
            if w is not None:
                deps[id(w)] = w
            if type(r) is tuple and r[0] == 'ps':
                rd = self.readers.get(r)
                if rd:
                    for kk, o in rd.items():
                        if kk != 'dma' and kk != eng:
                            deps[id(o)] = o
        for k in writes:
            w = self.last_w.get(k)
            if w is not None:
                deps[id(w)] = w
            rd = self.readers.get(k)
            if rd:
                for kk, o in rd.items():
                    if kk == 'dma':
                        for oo in o:
                            deps[id(oo)] = oo
                    else:
                        deps[id(o)] = o
        op.deps = list(deps.values())
        for r in reads:
            rd = self.readers.setdefault(r, {})
            if dma:
                rd.setdefault('dma', []).append(op)
            else:
                rd[eng] = op
        for k in writes:
            self.last_w[k] = op
            self.readers[k] = {}
        if dma:
            op.dma_n = self.dma_count[eng]
            self.dma_count[eng] += 1
        self.ops[eng].append(op)
        return op

    def pe(self, fn, reads=(), writes=()):
        return self.add('pe', fn, reads, writes)

    def act(self, fn, reads=(), writes=()):
        return self.add('act', fn, reads, writes)

    def dve(self, fn, reads=(), writes=()):
        return self.add('dve', fn, reads, writes)

    def pool(self, fn, reads=(), writes=()):
        return self.add('pool', fn, reads, writes)

    def dma(self, eng, fn, reads=(), writes=()):
        return self.add(eng, fn, reads, writes, dma=True)

    def _needs_wait(self, op, d):
        if d.is_dma:
            return True
        if d.eng == op.eng:
            if op.eng == 'pe' and not op.is_dma:
                return False
            return (op.idx - d.idx) <= 2
        return True

    def emit(self, nc, stack):
        engsem = {e: stack.enter_context(nc.semaphore("ms_" + e)) for e in ENGS}
        dmasem = {}
        for e in ENGS:
            if self.dma_count[e]:
                dmasem[e] = [stack.enter_context(nc.semaphore("dq_%s_%d" % (e, i)))
                             for i in range(min(self.R, self.dma_count[e]))]
        for e in ENGS:
            for op in self.ops[e]:
                for d in op.deps:
                    if not d.is_dma and self._needs_wait(op, d):
                        d.needs_inc = True
        for e in ENGS:
            cnt = 0
            for op in self.ops[e]:
                if op.is_dma:
                    ring = dmasem[e]
                    op.dma_sem = ring[op.dma_n % len(ring)]
                    op.dma_val = 16 * (op.dma_n // len(ring) + 1)
                elif op.needs_inc:
                    cnt += 1
                    op.ms = cnt
        final_dma = []
        for e in ENGS:
            n = self.dma_count[e]
            if n:
                ring = dmasem[e]
                for i, s in enumerate(ring):
                    cnt_i = (n - i + len(ring) - 1) // len(ring)
                    if cnt_i > 0:
                        final_dma.append((s, 16 * cnt_i))
        stats = {}

        def emit_engine(e, h):
            seen = {}
            nw = 0
            for op in self.ops[e]:
                needed = {}
                for d in op.deps:
                    if not self._needs_wait(op, d):
                        continue
                    if d.is_dma:
                        sem, val = d.dma_sem, d.dma_val
                    else:
                        sem, val = engsem[d.eng], d.ms
                    k = id(sem)
                    if needed.get(k, (None, 0))[1] < val:
                        needed[k] = (sem, val)
                if op.is_dma and op.dma_val > 16:
                    k = id(op.dma_sem)
                    if needed.get(k, (None, 0))[1] < op.dma_val - 16:
                        needed[k] = (op.dma_sem, op.dma_val - 16)
                for k, (sem, val) in needed.items():
                    if seen.get(k, 0) < val:
                        h.wait_ge(sem, val)
                        seen[k] = val
                        nw += 1
                inst = op.fn(h)
                if op.is_dma:
                    inst.then_inc(op.dma_sem, 16)
                elif op.needs_inc:
                    inst.then_inc(engsem[e], 1)
            if e == 'sp':
                for s, v in final_dma:
                    h.wait_ge(s, v)
            stats[e] = (len(self.ops[e]), nw)

        with nc.Block() as block:
            for e in ENGS:
                getattr(block, BLOCK_NAME[e])(lambda h, e=e: emit_engine(e, h))
        return stats


class RR:
    def __init__(self, items):
        self.items = list(items)
        self.i = 0

    def next(self):
        v = self.items[self.i % len(self.items)]
        self.i += 1
        return v


def build_nc(L=DEPTH, NSEQ=2, do_ffn1=True, do_mix=True, do_ffn2=True, do_final=True,
             do_rec=True, do_attn=True, debug=False, no_outproj=False):
    nc = bass.Bass("TRN2", target_bir_lowering=False)
    NV = NVL * L + 8
    xT_d = nc.dram_tensor("xT", [NSEQ, D, T], F32, kind="ExternalInput").ap()
    wu_d = nc.dram_tensor("wu", [L * UPL, 128, 1024], F32, kind="ExternalInput").ap()
    vecs_d = nc.dram_tensor("vecs", [128, NV], F32, kind="ExternalInput").ap()
    cs_d = nc.dram_tensor("cs", [2, 128, T], F32, kind="ExternalInput").ap()
    mask_d = nc.dram_tensor("maskT", [128, MASKW], F32, kind="ExternalInput").ap()
    pm_d = nc.dram_tensor("pm", [128, 128], F32, kind="ExternalInput").ap()
    out_d = nc.dram_tensor("outT", [NSEQ, D, T], F32, kind="ExternalOutput").ap()

    S = Sched()
    with ExitStack() as st:
        def sb(name, shape, dt):
            return st.enter_context(nc.sbuf_tensor(name, shape, dt))

        xT = sb("xT_sb", [128, KC, T], F32)
        hT = sb("hT_sb", [128, KC, T], BF16)
        mg = sb("mg_sb", [128, 4, T], BF16)
        cosT = sb("cosT", [128, T], F32)
        sinT = sb("sinT", [128, T], F32)
        maskT = sb("maskT_sb", [128, MASKW], BF16)
        vecs = sb("vecs_sb", [128, NV], F32)
        dvec = sb("dvec_sb", [128, L * 16], F32)
        ones = sb("ones_sb", [128, 128], BF16)
        pmat = sb("pmat_sb", [128, 128], BF16)
        hlast = sb("hlast_sb", [128, 2], F32)
        dummy = sb("dummy_sb", [128, 2], F32)
        wring = sb("wring_sb", [128, NU, 1024], BF16)
        SCRW = 11800
        scr = sb("scr_sb", [128, SCRW], F32)
        sp_t = [st.enter_context(nc.psum_tensor("spair%d" % i, [128, 1024], F32)) for i in range(3)]
        b67 = [st.enter_context(nc.psum_tensor("bank%d" % i, [128, 512], F32)) for i in (6, 7)]
        banks = []
        for i in range(3):
            banks.append(sp_t[i][:, 0:512])
            banks.append(sp_t[i][:, 512:1024])
        banks.append(b67[0][:])
        banks.append(b67[1][:])
        P0, P1, P2, P3 = RR([0, 1]), RR([2, 3]), RR([4, 5]), RR([6, 7])
        PY = RR([4, 5, 6, 7])
        PS4 = RR([0, 1, 2, 3])
        PS6 = RR([0, 1, 2, 3, 4, 5])
        SPR = RR([0, 1, 2])
        P45 = RR([4, 5])

        def f32v(off, n):
            return scr[:, off:off + n]

        def bf16v(off, n):
            return scr[:, off:off + n // 2].bitcast(BF16)

        SQ = [bf16v(0, 512), bf16v(256, 512)]
        LNV = f32v(512, 512)
        RSTD = [f32v(1024, 512), f32v(1536, 512)]
        CB = 2048
        SG = [f32v(CB + 0, 512), f32v(CB + 512, 512)]
        AB = [bf16v(CB + 1024, GS * 512), bf16v(CB + 1024 + GS * 256, GS * 512)]
        o = CB
        XB = []
        for b in range(2):
            XB.append(f32v(o, 1028)); o += 1028
        XR = []
        for b in range(2):
            XR.append(f32v(o, 1024)); o += 1024
        XR16 = []
        for b in range(2):
            XR16.append(bf16v(o, 1024)); o += 512
        RB = f32v(o, 1024); o += 1024
        IB = f32v(o, 1024); o += 1024
        ABUF = f32v(o, 1024); o += 1024
        GL = []
        for b in range(3):
            GL.append(bf16v(o, 1024)); o += 512
        assert o <= SCRW, o
        o = CB
        QT = []
        KT = []
        for b in range(1):
            QT.append(bf16v(o, T)); o += T // 2
            KT.append([bf16v(o, T), bf16v(o + T // 2, T)]); o += T
        VA = []
        for b in range(1):
            VA.append(bf16v(o, 16 * 2 * 128)); o += 16 * 2 * 64
        NET, NPT = 3, 3
        ET = []
        for b in range(NET):
            ET.append(bf16v(o, 1024)); o += 512
        PT = []
        for b in range(NPT):
            PT.append(bf16v(o, 1024)); o += 512
        T1 = f32v(o, 512); o += 512
        T2 = f32v(o, 512); o += 512
        LND = T1
        RCP = T2
        Q16 = []
        for b in range(2):
            Q16.append(bf16v(o, 512)); o += 256
        assert o <= SCRW, o

        fence_n = [0]

        def fence():
            fence_n[0] += 1
            S.dve(lambda h: h.memset(dummy[:, 0:1], 0.0), reads=['dummyw'], writes=['scr', 'dummyw'])

        class WStream:
            def __init__(self):
                self.loaded = 0
                self.cur = 0
                self.total = NSEQ * L * UPL

            def ensure(self, upto):
                upto = min(upto, self.total)
                while self.loaded < upto:
                    m = self.loaded
                    slot = m % NU
                    src = wu_d[m % (L * UPL)]
                    S.dma('pool', lambda h, slot=slot, src=src: h.dma_start(out=wring[:, slot, :], in_=src),
                          writes=[('w', slot)])
                    self.loaded += 1

            def take(self, n):
                a = self.cur
                self.ensure(a + n)
                res = []
                for m in range(a, a + n):
                    slot = m % NU
                    res.append((wring[:, slot, :], ('w', slot)))
                self.cur += n
                return res

            def skip(self, n):
                raise NotImplementedError

        W = WStream()

        S.dma('sp', lambda h: h.dma_start(out=vecs[:], in_=vecs_d), writes=['vecs'])
        S.dma('sp', lambda h: h.dma_start(out=cosT[:], in_=cs_d[0]), writes=['cos'])
        S.dma('sp', lambda h: h.dma_start(out=sinT[:], in_=cs_d[1]), writes=['sin'])
        S.dma('pool', lambda h: h.dma_start(out=maskT[:, 0:1216], in_=mask_d[:, 0:1216]), writes=['mask'])
        S.dma('pool', lambda h: h.dma_start(out=maskT[:, 1216:MASKW], in_=mask_d[:, 1216:MASKW]), writes=['mask'])
        S.dve(lambda h: h.memset(ones[:], 1.0), writes=['ones'])
        S.dma('pool', lambda h: h.dma_start(out=pmat[:], in_=pm_d), writes=['pmat'])
        S.dve(lambda h: h.memset(dummy[:], 0.0), writes=['dummyw'])
        S.dve(lambda h: h.memset(hlast[:], 0.0), writes=['hlast'])
        for l in range(L):
            lam = vecs[:, l * NVL + 52: l * NVL + 56]
            S.act(lambda h, l=l, lam=lam: h.activation(out=dvec[:, l * 16:l * 16 + 4], in_=lam, func=AF.Exp, scale=-1.0),
                  reads=['vecs'], writes=[('dv', l)])
            S.act(lambda h, l=l: h.activation(out=dvec[:, l * 16:l * 16 + 4], in_=dvec[:, l * 16:l * 16 + 4], func=AF.Ln, bias=1.0),
                  reads=[('dv', l)], writes=[('dv', l)])
            S.dve(lambda h, l=l: h.tensor_scalar(out=dvec[:, l * 16 + 12:l * 16 + 16], in0=dvec[:, l * 16:l * 16 + 4],
                                                 scalar1=-8.0, scalar2=None, op0=ALU.mult),
                  reads=[('dv', l)], writes=[('dvk', l)])
            S.dve(lambda h, l=l: h.tensor_scalar(out=dvec[:, l * 16:l * 16 + 4], in0=dvec[:, l * 16:l * 16 + 4],
                                                 scalar1=-4.0, scalar2=None, op0=ALU.mult),
                  reads=[('dv', l), ('dvk', l)], writes=[('dv', l)])
            S.dve(lambda h, l=l: h.tensor_scalar(out=dvec[:, l * 16 + 4:l * 16 + 12], in0=vecs[:, l * NVL + 44:l * NVL + 52],
                                                 scalar1=0.5, scalar2=None, op0=ALU.mult),
                  reads=['vecs', ('dv', l)], writes=[('dv', l)])

        def tsl(tt):
            return slice(tt * TT, (tt + 1) * TT)

        def rstd_of(chunk_aps, chunk_keys, nfeat, sq_rr, rstd_buf, rstd_key, stat_pool, scr_dep=True):
            bank = stat_pool.next()
            n = len(chunk_aps)
            for i, (ap, key) in enumerate(zip(chunk_aps, chunk_keys)):
                sqi = sq_rr.next()
                S.act(lambda h, ap=ap, sqi=sqi: h.activation(out=SQ[sqi], in_=ap, func=AF.Square),
                      reads=[key, 'scr'], writes=[('sq', sqi)])
                S.pe(lambda h, sqi=sqi, bank=bank, i=i, n=n: h.matmul(banks[bank][:], lhsT=ones[:], rhs=SQ[sqi],
                                                                         start=(i == 0), stop=(i == n - 1)),
                     reads=[('sq', sqi), 'ones', 'scr'], writes=[('ps', bank)])
            S.act(lambda h, bank=bank: h.activation(out=LNV, in_=banks[bank][:], func=AF.Ln,
                                                    scale=1.0 / nfeat, bias=EPS),
                  reads=[('ps', bank), 'scr'], writes=['lnv'])
            S.act(lambda h: h.activation(out=rstd_buf, in_=LNV, func=AF.Exp, scale=-0.5),
                  reads=['lnv', 'scr'], writes=[rstd_key])

        SQRR = RR([0, 1])
        RSRR = RR([0, 1])

        def norm_tile(gcol, tt):
            ri = RSRR.next()
            rstd_of([xT[:, c, tsl(tt)] for c in range(KC)], [('x', c, tt) for c in range(KC)],
                    D, SQRR, RSTD[ri], ('rstd', ri), P3)
            for c in range(KC):
                S.dve(lambda h, c=c, tt=tt, ri=ri: h.scalar_tensor_tensor(
                    out=hT[:, c, tsl(tt)], in0=xT[:, c, tsl(tt)], scalar=vecs[:, gcol + c:gcol + c + 1],
                    in1=RSTD[ri], op0=ALU.mult, op1=ALU.mult),
                    reads=[('x', c, tt), ('rstd', ri), 'vecs', 'scr'], writes=[('h', c, tt)])

        normed = {'g': None}

        def norm_to_hT(gcol):
            if normed['g'] == gcol:
                return
            for tt in range(NTT):
                norm_tile(gcol, tt)
            normed['g'] = gcol

        def make_post(gnext):
            if gnext is None:
                return None

            def post(tt):
                norm_tile(gnext, tt)
                if tt == NTT - 1:
                    normed['g'] = gnext
            return post

        def ffn(gcol, post_tile=None):
            norm_to_hT(gcol)
            normed['g'] = None
            groups = [list(range(j, min(j + GS, NFC))) for j in range(0, NFC, GS)]
            its = []
            for gi, g in enumerate(groups):
                for tt in range(NTT):
                    its.append((gi, g, tt))
            gunits = {}
            sgrr = RR([0, 1])
            abrr = RR([0, 1])
            state = {}

            def GU(n):
                gi, g, tt = its[n]
                if gi not in gunits:
                    gunits[gi] = W.take(3 * len(g))
                units = gunits[gi]
                ab = abrr.next()
                state[n] = ab
                for jj in range(len(g)):
                    (wg, kg), (wu_, ku), _ = units[3 * jj:3 * jj + 3]
                    bG = P0.next()
                    bU = P1.next()
                    for k in range(KC):
                        S.pe(lambda h, wg=wg, k=k, bG=bG, tt=tt: h.matmul(
                            banks[bG][:], lhsT=wg[:, k * 128:(k + 1) * 128], rhs=hT[:, k, tsl(tt)],
                            start=(k == 0), stop=(k == KC - 1)),
                            reads=[kg, ('h', k, tt)], writes=[('ps', bG)])
                    for k in range(KC):
                        S.pe(lambda h, wu_=wu_, k=k, bU=bU, tt=tt: h.matmul(
                            banks[bU][:], lhsT=wu_[:, k * 128:(k + 1) * 128], rhs=hT[:, k, tsl(tt)],
                            start=(k == 0), stop=(k == KC - 1)),
                            reads=[ku, ('h', k, tt)], writes=[('ps', bU)])
                    si = sgrr.next()
                    S.act(lambda h, si=si, bG=bG: h.activation(out=SG[si], in_=banks[bG][:], func=AF.Silu),
                          reads=[('ps', bG), 'scr'], writes=[('sg', si)])
                    S.dve(lambda h, si=si, bU=bU, ab=ab, jj=jj: h.tensor_tensor(
                        out=AB[ab][:, jj * 512:(jj + 1) * 512], in0=banks[bU][:], in1=SG[si], op=ALU.mult),
                        reads=[('ps', bU), ('sg', si), 'scr'], writes=[('ab', ab, jj)])

            def Y(n):
                gi, g, tt = its[n]
                units = gunits[gi]
                ab = state.pop(n)
                for dc in range(KC):
                    bY = PY.next()
                    for jj in range(len(g)):
                        wo, ko = units[3 * jj + 2]
                        S.pe(lambda h, wo=wo, dc=dc, bY=bY, ab=ab, jj=jj, ng=len(g): h.matmul(
                            banks[bY][:], lhsT=wo[:, dc * 128:(dc + 1) * 128], rhs=AB[ab][:, jj * 512:(jj + 1) * 512],
                            start=(jj == 0), stop=(jj == ng - 1)),
                            reads=[ko, ('ab', ab, jj), 'scr'], writes=[('ps', bY)])
                    S.dve(lambda h, dc=dc, bY=bY, tt=tt: h.scalar_tensor_tensor(
                        out=xT[:, dc, tsl(tt)], in0=banks[bY][:], scalar=0.5, in1=xT[:, dc, tsl(tt)],
                        op0=ALU.mult, op1=ALU.add),
                        reads=[('ps', bY), ('x', dc, tt)], writes=[('x', dc, tt)])

            GU(0)
            for n in range(len(its)):
                if n + 1 < len(its):
                    GU(n + 1)
                Y(n)
                if post_tile is not None and its[n][0] == len(groups) - 1 and its[n][2] >= 1:
                    post_tile(its[n][2] - 1)
            if post_tile is not None:
                post_tile(NTT - 1)

        def out_proj(gcol, nfeat_units, post_tile=None):
            def stage1(tt):
                ri = RSRR.next()
                rstd_of([mg[:, c, tsl(tt)] for c in range(4)], [('mg', c, tt) for c in range(4)],
                        512, SQRR, RSTD[ri], ('rstd', ri), P0)
                for c in range(4):
                    S.dve(lambda h, c=c, tt=tt, ri=ri: h.scalar_tensor_tensor(
                        out=mg[:, c, tsl(tt)], in0=mg[:, c, tsl(tt)], scalar=vecs[:, gcol + c:gcol + c + 1],
                        in1=RSTD[ri], op0=ALU.mult, op1=ALU.mult),
                        reads=[('mg', c, tt), ('rstd', ri), 'vecs', 'scr'], writes=[('mg', c, tt)])

            def stage2(tt):
                for dc in range(KC):
                    bY = P2.next()
                    for c in range(4):
                        wo, ko = nfeat_units[c]
                        S.pe(lambda h, wo=wo, dc=dc, bY=bY, c=c, tt=tt: h.matmul(
                            banks[bY][:], lhsT=wo[:, dc * 128:(dc + 1) * 128], rhs=mg[:, c, tsl(tt)],
                            start=(c == 0), stop=(c == 3)),
                            reads=[ko, ('mg', c, tt)], writes=[('ps', bY)])
                    S.dve(lambda h, dc=dc, bY=bY, tt=tt: h.tensor_tensor(
                        out=xT[:, dc, tsl(tt)], in0=banks[bY][:], in1=xT[:, dc, tsl(tt)], op=ALU.add),
                        reads=[('ps', bY), ('x', dc, tt)], writes=[('x', dc, tt)])

            stage1(0)
            for tt in range(NTT):
                if tt + 1 < NTT:
                    stage1(tt + 1)
                stage2(tt)
                if post_tile is not None and tt >= 1:
                    post_tile(tt - 1)
            if post_tile is not None:
                post_tile(NTT - 1)

        def rec_branch(l):
            vb = l * NVL
            units_bd = W.take(1)
            wbd, kbd = units_bd[0]
            pieces = [(rc, hf) for rc in range(4) for hf in range(2)]
            runits = {}

            def vec_aps(rc):
                cw = [vecs[:, vb + 24 + j * 4 + rc: vb + 24 + j * 4 + rc + 1] for j in range(4)]
                cb = vecs[:, vb + 40 + rc: vb + 41 + rc]
                ba = dvec[:, l * 16 + 4 + rc: l * 16 + 5 + rc]
                bx = dvec[:, l * 16 + 8 + rc: l * 16 + 9 + rc]
                kl = dvec[:, l * 16 + rc: l * 16 + rc + 1]
                kl2 = dvec[:, l * 16 + 12 + rc: l * 16 + 13 + rc]
                return cw, cb, ba, bx, kl, kl2

            def stA(p):
                rc, hf = pieces[p]
                if p == 0:
                    runits[0] = W.take(2)
                    runits[1] = W.take(2)
                elif hf == 0 and rc + 1 < 4:
                    runits[rc + 1] = W.take(2)
                (wxb, kxb), (wgb, kgb) = runits[rc]
                xi, gi = p % 2, p % 3
                xb, gl = XB[xi], GL[gi]
                if hf == 0:
                    S.dve(lambda h, xb=xb: h.memset(xb[:, 0:3], 0.0), reads=['scr'], writes=[('xb', xi)])
                else:
                    xprev = XB[(p - 1) % 2]
                    S.dve(lambda h, xb=xb, xprev=xprev: h.tensor_copy(out=xb[:, 0:3], in_=xprev[:, 1024:1027]),
                          reads=[('xb', (p - 1) % 2), 'scr'], writes=[('xb', xi)])
                for t2 in range(2):
                    tt = hf * 2 + t2
                    b0 = PS4.next()
                    for k in range(KC):
                        S.pe(lambda h, k=k, b0=b0, tt=tt, wxb=wxb: h.matmul(
                            banks[b0][:], lhsT=wxb[:, k * 128:(k + 1) * 128], rhs=hT[:, k, tsl(tt)],
                            start=(k == 0), stop=(k == KC - 1)),
                            reads=[kxb, ('h', k, tt)], writes=[('ps', b0)])
                    S.act(lambda h, b0=b0, t2=t2, xb=xb: h.activation(out=xb[:, 3 + t2 * 512:3 + (t2 + 1) * 512],
                                                                      in_=banks[b0][:], func=AF.Copy),
                          reads=[('ps', b0), 'scr'], writes=[('xb', xi)])
                    b1 = PS4.next()
                    for k in range(KC):
                        S.pe(lambda h, k=k, b1=b1, tt=tt, wgb=wgb: h.matmul(
                            banks[b1][:], lhsT=wgb[:, k * 128:(k + 1) * 128], rhs=hT[:, k, tsl(tt)],
                            start=(k == 0), stop=(k == KC - 1)),
                            reads=[kgb, ('h', k, tt)], writes=[('ps', b1)])
                    S.act(lambda h, b1=b1, t2=t2, gl=gl: h.activation(out=gl[:, t2 * 512:(t2 + 1) * 512],
                                                                      in_=banks[b1][:], func=AF.Gelu_apprx_tanh),
                          reads=[('ps', b1), 'scr'], writes=[('gl', gi)])

            def stB(p):
                rc, hf = pieces[p]
                cw, cb, ba, bx, kl, kl2 = vec_aps(rc)
                xi = p % 2
                xb, xr, xr16 = XB[xi], XR[xi], XR16[xi]
                S.pool(lambda h, cw=cw, cb=cb, xb=xb, xr=xr: h.tensor_scalar(
                    out=xr, in0=xb[:, 3:1027], scalar1=cw[3], scalar2=cb, op0=ALU.mult, op1=ALU.add),
                    reads=[('xb', xi), 'vecs', 'scr'], writes=[('xr', xi)])
                for j in (2, 1, 0):
                    S.dve(lambda h, cw=cw, j=j, xb=xb, xr=xr: h.scalar_tensor_tensor(
                        out=xr, in0=xb[:, j:j + 1024], scalar=cw[j], in1=xr, op0=ALU.mult, op1=ALU.add),
                        reads=[('xb', xi), ('xr', xi), 'vecs', 'scr'], writes=[('xr', xi)])
                S.act(lambda h, xr=xr, xr16=xr16: h.activation(out=xr16, in_=xr, func=AF.Copy),
                      reads=[('xr', xi), 'scr'], writes=[('xr16', xi)])

            def stC(p):
                rc, hf = pieces[p]
                cw, cb, ba, bx, kl, kl2 = vec_aps(rc)
                xi = p % 2
                xr16 = XR16[xi]
                for t2 in range(2):
                    ba_ = P45.next()
                    S.pe(lambda h, ba_=ba_, t2=t2, rc=rc, xr16=xr16: h.matmul(
                        banks[ba_][:], lhsT=wbd[:, rc * 128:(rc + 1) * 128], rhs=xr16[:, t2 * 512:(t2 + 1) * 512],
                        start=True, stop=True), reads=[kbd, ('xr16', xi), 'scr'], writes=[('ps', ba_)])
                    S.act(lambda h, ba_=ba_, t2=t2, ba=ba: h.activation(
                        out=RB[:, t2 * 512:(t2 + 1) * 512], in_=banks[ba_][:], func=AF.Tanh, bias=ba, scale=0.5),
                        reads=[('ps', ba_), ('dv', l), 'scr'], writes=['rb'])
                    bx_ = P45.next()
                    S.pe(lambda h, bx_=bx_, t2=t2, rc=rc, xr16=xr16: h.matmul(
                        banks[bx_][:], lhsT=wbd[:, 512 + rc * 128:512 + (rc + 1) * 128],
                        rhs=xr16[:, t2 * 512:(t2 + 1) * 512], start=True, stop=True),
                        reads=[kbd, ('xr16', xi), 'scr'], writes=[('ps', bx_)])
                    S.act(lambda h, bx_=bx_, t2=t2, bx=bx: h.activation(
                        out=IB[:, t2 * 512:(t2 + 1) * 512], in_=banks[bx_][:], func=AF.Tanh, bias=bx, scale=0.5),
                        reads=[('ps', bx_), ('dv', l), 'scr'], writes=['ib'])
                S.act(lambda h, kl=kl: h.activation(out=ABUF, in_=RB, func=AF.Exp, scale=kl, bias=kl),
                      reads=['rb', ('dv', l), 'scr'], writes=['abuf'])
                S.act(lambda h, kl2=kl2: h.activation(out=RB, in_=RB, func=AF.Exp, scale=kl2, bias=kl2),
                      reads=['rb', ('dvk', l), 'scr'], writes=['rb'])
                S.act(lambda h: h.activation(out=RB, in_=RB, func=AF.Sqrt, scale=-1.0, bias=1.0),
                      reads=['rb', 'scr'], writes=['rb'])

            def stD(p):
                rc, hf = pieces[p]
                xi, gi = p % 2, p % 3
                xr, gl = XR[xi], GL[gi]
                S.dve(lambda h: h.scalar_tensor_tensor(out=IB, in0=IB, scalar=1.0, in1=RB, op0=ALU.add, op1=ALU.mult),
                      reads=['ib', 'rb', 'scr'], writes=['ib'])
                S.dve(lambda h, xr=xr: h.scalar_tensor_tensor(out=IB, in0=IB, scalar=0.5, in1=xr, op0=ALU.mult, op1=ALU.mult),
                      reads=['ib', ('xr', xi), 'scr'], writes=['ib'])
                if hf == 0:
                    S.dve(lambda h, xr=xr: h.tensor_tensor_scan(out=xr, data0=ABUF, data1=IB, initial=0.0,
                                                                op0=ALU.mult, op1=ALU.add),
                          reads=['abuf', 'ib', ('xr', xi), 'scr'], writes=[('xr', xi)])
                    S.dve(lambda h, xr=xr: h.tensor_copy(out=hlast[:, 0:1], in_=xr[:, 1023:1024]),
                          reads=[('xr', xi), 'scr'], writes=['hlast'])
                else:
                    S.dve(lambda h, xr=xr: h.tensor_tensor_scan(out=xr, data0=ABUF, data1=IB, initial=hlast[:, 0:1],
                                                                op0=ALU.mult, op1=ALU.add),
                          reads=['abuf', 'ib', ('xr', xi), 'hlast', 'scr'], writes=[('xr', xi)])
                S.pool(lambda h, rc=rc, hf=hf, xr=xr, gl=gl: h.tensor_tensor(
                    out=mg[:, rc, hf * 1024:(hf + 1) * 1024], in0=xr, in1=gl, op=ALU.mult),
                    reads=[('xr', xi), ('gl', gi), 'scr'], writes=[('mg', rc, 2 * hf), ('mg', rc, 2 * hf + 1)])

            NP = len(pieces)
            stA(0)
            stA(1)
            stB(0)
            for p in range(NP):
                stC(p)
                if p + 2 < NP:
                    stA(p + 2)
                if p + 1 < NP:
                    stB(p + 1)
                stD(p)
            wo_units = W.take(4)
            out_proj(l * NVL + 60, wo_units)

        def attn_branch(l, post_tile=None):
            vb = l * NVL
            qkrr = RR([0])
            varr = RR([0])
            etrr = RR(list(range(NET)))
            ptrr = RR(list(range(NPT)))
            for hp in range(4):
                (wv, kv), (wq, kq), (wk, kk_) = W.take(3)
                qb = qkrr.next()
                vbuf = varr.next()
                VAv = VA[vbuf].rearrange("p (kb e f) -> p kb e f", kb=16, e=2)
                for kq4 in range(4):
                    bV = PS6.next()
                    for q4 in range(4):
                        kb = kq4 * 4 + q4
                        for k in range(KC):
                            S.pe(lambda h, k=k, kb=kb, q4=q4, bV=bV, wv=wv: h.matmul(
                                banks[bV][:, q4 * 128:(q4 + 1) * 128], lhsT=hT[:, k, kb * 128:(kb + 1) * 128],
                                rhs=wv[:, k * 128:(k + 1) * 128], start=(k == 0), stop=(k == KC - 1)),
                                reads=[kv, ('h', k, kb // 4)], writes=[('ps', bV)])
                    bview = banks[bV][:].rearrange("p (q e f) -> p q e f", q=4, e=2)
                    S.act(lambda h, bview=bview, VAv=VAv, kq4=kq4: h.activation(
                        out=VAv[:, kq4 * 4:(kq4 + 1) * 4, 0, 0:64], in_=bview[:, :, 0, :], func=AF.Copy),
                        reads=[('ps', bV), 'scr'], writes=[('va', vbuf, kq4, 0)])
                    S.act(lambda h, bview=bview, VAv=VAv, kq4=kq4: h.activation(
                        out=VAv[:, kq4 * 4:(kq4 + 1) * 4, 1, 64:128], in_=bview[:, :, 1, :], func=AF.Copy),
                        reads=[('ps', bV), 'scr'], writes=[('va', vbuf, kq4, 1)])
                tiles = []
                for (wa, ka, dst, dkey) in ((wq, kq, QT[qb], 'qt'), (wk, kk_, KT[qb], 'kt')):
                    for tt in range(NTT):
                        tiles.append((wa, ka, dst, dkey, tt))
                tstate = {}

                def RP(i):
                    wa, ka, dst, dkey, tt = tiles[i]
                    bA = PS6.next()
                    qi = i % 2
                    for k in range(KC):
                        S.pe(lambda h, k=k, bA=bA, tt=tt, wa=wa: h.matmul(
                            banks[bA][:], lhsT=wa[:, k * 128:(k + 1) * 128], rhs=hT[:, k, tsl(tt)],
                            start=(k == 0), stop=(k == KC - 1)),
                            reads=[ka, ('h', k, tt)], writes=[('ps', bA)])
                    S.act(lambda h, bA=bA, qi=qi: h.activation(out=Q16[qi], in_=banks[bA][:], func=AF.Copy),
                          reads=[('ps', bA), 'scr'], writes=[('q16', qi)])
                    tstate[i] = bA

                def RR_(i):
                    wa, ka, dst, dkey, tt = tiles[i]
                    bA = tstate.pop(i)
                    qi = i % 2
                    bB = PS6.next()
                    S.pe(lambda h, bB=bB, qi=qi: h.matmul(banks[bB][:], lhsT=pmat[:], rhs=Q16[qi], start=True, stop=True),
                         reads=['pmat', ('q16', qi), 'scr'], writes=[('ps', bB)])
                    S.dve(lambda h, bA=bA, tt=tt: h.tensor_tensor(out=T1, in0=banks[bA][:], in1=cosT[:, tsl(tt)], op=ALU.mult),
                          reads=[('ps', bA), 'cos', 'scr'], writes=['t1'])
                    S.dve(lambda h, bB=bB, tt=tt: h.tensor_tensor(out=T2, in0=banks[bB][:], in1=sinT[:, tsl(tt)], op=ALU.mult),
                          reads=[('ps', bB), 'sin', 'scr'], writes=['t2'])
                    if dkey == 'qt':
                        S.dve(lambda h, dst=dst, tt=tt: h.tensor_tensor(out=dst[:, tsl(tt)], in0=T1, in1=T2, op=ALU.add),
                              reads=['t1', 't2', 'scr'], writes=[(dkey, qb, tt)])
                    else:
                        for e in range(2):
                            rows = slice(e * 64, (e + 1) * 64)
                            S.dve(lambda h, dst=dst, tt=tt, e=e, rows=rows: h.tensor_tensor(
                                out=dst[e][rows, tsl(tt)], in0=T1[rows, :], in1=T2[rows, :], op=ALU.add),
                                reads=['t1', 't2', 'ktz', 'scr'], writes=[(dkey, qb, tt, e)])

                RP(0)
                for i in range(len(tiles)):
                    if i + 1 < len(tiles):
                        RP(i + 1)
                    RR_(i)
                pairs = []
                for e in range(2):
                    for g in range(NTT):
                        for kp in range((4 * g + 4) // 2):
                            pairs.append((e, g, 2 * kp))
                ust = {}

                def qlo_of(g, kb):
                    return 128 * max(0, kb - 4 * g)

                def SUP(n):
                    e, g, kb0 = pairs[n]
                    sp = SPR.next()
                    kbA, kbB = kb0 + 1, kb0
                    qa, qb_ = qlo_of(g, kbA), qlo_of(g, kbB)
                    wA, wB = 512 - qa, 512 - qb_
                    S.pe(lambda h, sp=sp, e=e, kb=kbA, g=g, qb=qb, qa=qa: h.matmul(
                        sp_t[sp][:, qa:512], lhsT=KT[qb][e][:, kb * 128:(kb + 1) * 128],
                        rhs=QT[qb][:, g * 512 + qa:(g + 1) * 512], start=True, stop=True),
                        reads=[('kt', qb, kbA // 4, e), 'ktz', ('qt', qb, g), 'scr'], writes=[('ps', 2 * sp)])
                    S.pe(lambda h, sp=sp, e=e, kb=kbB, g=g, qb=qb, qb_=qb_, wB=wB: h.matmul(
                        sp_t[sp][:, 512:512 + wB], lhsT=KT[qb][e][:, kb * 128:(kb + 1) * 128],
                        rhs=QT[qb][:, g * 512 + qb_:(g + 1) * 512], start=True, stop=True),
                        reads=[('kt', qb, kbB // 4, e), 'ktz', ('qt', qb, g), 'scr'], writes=[('ps', 2 * sp + 1)])
                    ei = etrr.next()
                    S.act(lambda h, sp=sp, ei=ei, qa=qa, wB=wB: h.activation(
                        out=ET[ei][:, qa:512 + wB], in_=sp_t[sp][:, qa:512 + wB], func=AF.Exp, scale=0.125),
                        reads=[('ps', 2 * sp), ('ps', 2 * sp + 1), 'scr'], writes=[('et', ei)])
                    pi = ptrr.next()
                    offA = 512 * g - 128 * kbA + 384
                    offB = 512 * g - 128 * kbB + 384
                    if qa == 0 and qb_ == 0:
                        mb = maskT[:, offA:offA + 512]
                        m_ap = bass.AP(mb.tensor, mb.offset, [list(mb.ap[0]), [128, 2], [1, 512]])
                        S.dve(lambda h, ei=ei, pi=pi, m_ap=m_ap: h.tensor_tensor(
                            out=PT[pi].rearrange("p (a b) -> p a b", a=2), in0=ET[ei].rearrange("p (a b) -> p a b", a=2),
                            in1=m_ap, op=ALU.mult),
                            reads=[('et', ei), 'mask', 'scr'], writes=[('pt', pi)])
                    else:
                        S.dve(lambda h, ei=ei, pi=pi, offA=offA, qa=qa: h.tensor_tensor(
                            out=PT[pi][:, qa:512], in0=ET[ei][:, qa:512], in1=maskT[:, offA + qa:offA + 512], op=ALU.mult),
                            reads=[('et', ei), 'mask', 'scr'], writes=[('pt', pi)])
                        S.dve(lambda h, ei=ei, pi=pi, offB=offB, qb_=qb_, wB=wB: h.tensor_tensor(
                            out=PT[pi][:, 512:512 + wB], in0=ET[ei][:, 512:512 + wB],
                            in1=maskT[:, offB + qb_:offB + 512], op=ALU.mult),
                            reads=[('et', ei), 'mask', 'scr'], writes=[('pt', pi)])
                    ust[n] = (pi, qa, qb_, wB)

                obank = {}

                def PVP(n):
                    e, g, kb0 = pairs[n]
                    pi, qa, qb_, wB = ust.pop(n)
                    first = (kb0 == 0)
                    last = (kb0 + 2 == 4 * g + 4)
                    if first:
                        obank[(e, g)] = P3.next()
                    bO = obank[(e, g)]
                    S.pe(lambda h, bO=bO, kb0=kb0, e=e, pi=pi, first=first, VAv=VAv, qa=qa: h.matmul(
                        banks[bO][:, qa:512], lhsT=VAv[:, kb0 + 1, e, :], rhs=PT[pi][:, qa:512], start=first, stop=False),
                        reads=[('va', vbuf, (kb0 + 1) // 4, 0), ('va', vbuf, (kb0 + 1) // 4, 1), 'vaones', ('pt', pi), 'scr'],
                        writes=[('ps', bO)])
                    S.pe(lambda h, bO=bO, kb0=kb0, e=e, pi=pi, last=last, VAv=VAv, qb_=qb_, wB=wB: h.matmul(
                        banks[bO][:, qb_:512], lhsT=VAv[:, kb0, e, :], rhs=PT[pi][:, 512:512 + wB], start=False, stop=last),
                        reads=[('va', vbuf, kb0 // 4, 0), ('va', vbuf, kb0 // 4, 1), 'vaones', ('pt', pi), 'scr'],
                        writes=[('ps', bO)])
                    if last:
                        nrows = slice(e * 64, (e + 1) * 64)
                        drows = slice((1 - e) * 64, (2 - e) * 64)
                        S.act(lambda h, bO=bO, drows=drows: h.activation(out=LND[drows, :], in_=banks[bO][drows, :], func=AF.Ln),
                              reads=[('ps', bO), 'scr'], writes=['t1'])
                        S.act(lambda h, drows=drows: h.activation(out=RCP[drows, :], in_=LND[drows, :], func=AF.Exp, scale=-1.0),
                              reads=['t1', 'scr'], writes=['t2'])
                        S.dve(lambda h, bO=bO, nrows=nrows, drows=drows, g=g, hp=hp: h.tensor_tensor(
                            out=mg[nrows, hp, tsl(g)], in0=banks[bO][nrows, :], in1=RCP[drows, :], op=ALU.mult),
                            reads=[('ps', bO), 't2', 'scr'], writes=[('mg', hp, g)])

                LOOK = 2
                for n in range(min(LOOK, len(pairs))):
                    SUP(n)
                for n in range(len(pairs)):
                    if n + LOOK < len(pairs):
                        SUP(n + LOOK)
                    PVP(n)
            wo_units = W.take(4)
            if not no_outproj:
                out_proj(l * NVL + 56, wo_units, post_tile)

        def set_va_ones():
            for b in range(1):
                VAv0 = VA[b].rearrange("p (kb e f) -> p kb e f", kb=16, e=2)
                S.dve(lambda h, VAv0=VAv0: h.memset(VAv0[:, :, 0, 64:128], 1.0), reads=['scr'], writes=['vaones'])
                S.dve(lambda h, VAv0=VAv0: h.memset(VAv0[:, :, 1, 0:64], 1.0), reads=['scr'], writes=['vaones'])
                S.dve(lambda h, b=b: h.memset(KT[b][0][64:128, :], 0.0), reads=['scr'], writes=['ktz'])
                S.dve(lambda h, b=b: h.memset(KT[b][1][0:64, :], 0.0), reads=['scr'], writes=['ktz'])

        for s in range(NSEQ):
            xsrc = xT_d[s].rearrange("(c p) t -> p c t", p=128)
            for tt in range(NTT):
                S.dma('sp', lambda h, tt=tt, xsrc=xsrc: h.dma_start(out=xT[:, :, tsl(tt)], in_=xsrc[:, :, tsl(tt)]),
                      writes=[('x', c, tt) for c in range(KC)])
            phases = []
            for l in range(L):
                vb = l * NVL
                phases.append(('ffn1', l, vb + 0, do_ffn1))
                phases.append(('mix', l, vb + 8, do_mix))
                phases.append(('ffn2', l, vb + 16, do_ffn2))
            normed['g'] = None
            for pi_, (kind, l, gcol, on) in enumerate(phases):
                gnext = None
                for (k2, l2, g2, on2) in phases[pi_ + 1:]:
                    if on2:
                        gnext = g2
                        break
                post = make_post(gnext)
                if kind in ('ffn1', 'ffn2'):
                    if on:
                        ffn(gcol, post)
                    else:
                        W.take(66)
                else:
                    if on:
                        fence()
                        norm_to_hT(gcol)
                        normed['g'] = None
                        if do_rec:
                            rec_branch(l)
                        else:
                            W.take(13)
                        fence()
                        if do_attn:
                            set_va_ones()
                            attn_branch(l, post)
                        else:
                            W.take(16)
                        fence()
                    else:
                        W.take(29)
            odst = out_d[s].rearrange("(c p) t -> p c t", p=128)
            gcol = NVL * L
            for tt in range(NTT):
                if do_final:
                    ri = RSRR.next()
                    rstd_of([xT[:, c, tsl(tt)] for c in range(KC)], [('x', c, tt) for c in range(KC)],
                            D, SQRR, RSTD[ri], ('rstd', ri), P3)
                    for c in range(KC):
                        S.dve(lambda h, c=c, tt=tt, ri=ri: h.scalar_tensor_tensor(
                            out=xT[:, c, tsl(tt)], in0=xT[:, c, tsl(tt)], scalar=vecs[:, gcol + c:gcol + c + 1],
                            in1=RSTD[ri], op0=ALU.mult, op1=ALU.mult),
                            reads=[('x', c, tt), ('rstd', ri), 'vecs', 'scr'], writes=[('x', c, tt)])
                S.dma('sp', lambda h, tt=tt, odst=odst: h.dma_start(out=odst[:, :, tsl(tt)], in_=xT[:, :, tsl(tt)]),
                      reads=[('x', c, tt) for c in range(KC)])
        if debug:
            dh = nc.dram_tensor("dbg_h", [128, KC * T], BF16, kind="ExternalOutput").ap()
            dm = nc.dram_tensor("dbg_mg", [128, 4 * T], BF16, kind="ExternalOutput").ap()
            ds = nc.dram_tensor("dbg_scr", [128, SCRW], F32, kind="ExternalOutput").ap()
            dw = nc.dram_tensor("dbg_w", [128, NU * 1024], BF16, kind="ExternalOutput").ap()
            S.dma('sp', lambda h: h.dma_start(out=dh, in_=hT[:].rearrange("p c t -> p (c t)")),
                  reads=[('h', c, tt) for c in range(KC) for tt in range(NTT)])
            S.dma('sp', lambda h: h.dma_start(out=dm, in_=mg[:].rearrange("p c t -> p (c t)")),
                  reads=[('mg', c, tt) for c in range(4) for tt in range(NTT)])
            S.dve(lambda h: h.memset(dummy[:, 1:2], 0.0), reads=['scr'], writes=['scrdump'])
            S.dma('sp', lambda h: h.dma_start(out=ds, in_=scr[:]), reads=['scrdump'])
            S.dma('sp', lambda h: h.dma_start(out=dw, in_=wring[:].rearrange("p u f -> p (u f)")),
                  reads=[('w', u) for u in range(NU)])
        stats = S.emit(nc, st)
    return nc, stats


def _colunit(Wm, cols):
    return np.ascontiguousarray(Wm[:, cols].reshape(8, 128, 128).transpose(1, 0, 2).reshape(128, 1024))


def make_units(inp, L):
    units = np.empty((L * UPL, 128, 1024), np.float32)
    ar = np.arange(128)
    u = 0
    for l in range(L):
        for pre in ("ffn1", "mix", "ffn2"):
            if pre != "mix":
                w_in = inp[pre + "_w_in"][l]
                w_out = inp[pre + "_w_out"][l]
                for j in range(NFC):
                    units[u] = _colunit(w_in, j * 128 + ar); u += 1
                    units[u] = _colunit(w_in, DFF + j * 128 + ar); u += 1
                    units[u] = w_out[j * 128:(j + 1) * 128, :]; u += 1
            else:
                w_in = inp["w_in"][l]
                w_out = inp["w_out"][l]
                bd = np.zeros((128, 1024), np.float32)
                for gi, nm in enumerate(("rg_w_a", "rg_w_x")):
                    wg = inp[nm][l]
                    for rc in range(4):
                        c0 = gi * 512 + rc * 128
                        bd[0:64, c0:c0 + 64] = wg[2 * rc]
                        bd[64:128, c0 + 64:c0 + 128] = wg[2 * rc + 1]
                units[u] = bd; u += 1
                for rc in range(4):
                    units[u] = _colunit(w_in, 1536 + rc * 128 + ar); u += 1
                    units[u] = _colunit(w_in, 2048 + rc * 128 + ar); u += 1
                for rc in range(4):
                    units[u] = w_out[512 + rc * 128:512 + (rc + 1) * 128, :]; u += 1
                for hp in range(4):
                    units[u] = _colunit(w_in, 1024 + hp * 128 + ar); u += 1
                    units[u] = _colunit(w_in, hp * 128 + ar); u += 1
                    units[u] = _colunit(w_in, 512 + hp * 128 + ar); u += 1
                for hp in range(4):
                    units[u] = w_out[hp * 128:(hp + 1) * 128, :]; u += 1
    assert u == L * UPL
    return units


def make_vecs(inp, L):
    NV = NVL * L + 8
    v = np.zeros((128, NV), np.float32)

    def cols(vec, n):
        return np.asarray(vec, np.float32).reshape(n, 128).T

    for l in range(L):
        b = l * NVL
        v[:, b + 0:b + 8] = cols(inp["ffn1_norm"][l], 8)
        v[:, b + 8:b + 16] = cols(inp["mix_norm"][l], 8)
        v[:, b + 16:b + 24] = cols(inp["ffn2_norm"][l], 8)
        for j in range(4):
            v[:, b + 24 + 4 * j:b + 28 + 4 * j] = cols(inp["conv_w"][l][j], 4)
        v[:, b + 40:b + 44] = cols(inp["conv_b"][l], 4)
        v[:, b + 44:b + 48] = cols(inp["rg_b_a"][l], 4)
        v[:, b + 48:b + 52] = cols(inp["rg_b_x"][l], 4)
        v[:, b + 52:b + 56] = cols(inp["rg_lambda"][l], 4)
        v[:, b + 56:b + 60] = cols(inp["attn_out_norm"][l], 4)
        v[:, b + 60:b + 64] = cols(inp["rec_out_norm"][l], 4)
    v[:, NVL * L:NVL * L + 8] = cols(inp["final_norm"], 8)
    return v


def make_tables():
    pos = np.arange(T, dtype=np.float32)
    inv = (np.float32(10000.0) ** (-np.arange(0, 64, 2, dtype=np.float32) / np.float32(64))).astype(np.float32)
    ang = (pos[:, None] * inv[None, :]).astype(np.float32)
    cos = np.cos(ang).astype(np.float32).T
    sin = np.sin(ang).astype(np.float32).T
    p = np.arange(128)
    dd = p % 64
    f = dd % 32
    cs = np.empty((2, 128, T), np.float32)
    cs[0] = cos[f]
    cs[1] = np.where((dd < 32)[:, None], -sin[f], sin[f])
    xx = np.arange(MASKW)[None, :]
    jj = np.arange(128)[:, None]
    dist = xx - jj - 384
    c = ((dist >= 0) & (dist <= 128)).astype(np.float32)
    c += ((dist >= 0) & (dist % 4 == 0) & (dist <= 512)).astype(np.float32)
    c += ((dist >= 0) & (dist % 16 == 0) & (dist <= 2048)).astype(np.float32)
    ar = np.arange(128)
    d = ar % 64
    perm = (ar // 64) * 64 + np.where(d < 32, d + 32, d - 32)
    pm = np.zeros((128, 128), np.float32)
    pm[perm, ar] = 1.0
    return cs, c.astype(np.float32), pm


_CACHE = {}


def kernel(**inputs):
    inp = {k: np.asarray(v) for k, v in inputs.items()}
    x = inp["x"].astype(np.float32, copy=False)
    B = x.shape[0]
    n_cores = 8
    nseq = B // n_cores
    L = inp["ffn1_norm"].shape[0]
    key = (L, nseq)
    if key not in _CACHE:
        _CACHE[key] = build_nc(L=L, NSEQ=nseq)[0]
    nc = _CACHE[key]
    units = make_units(inp, L)
    vecs = make_vecs(inp, L)
    cs, maskT, pm = make_tables()
    in_maps = []
    for c in range(n_cores):
        xs = np.ascontiguousarray(x[c * nseq:(c + 1) * nseq].transpose(0, 2, 1))
        in_maps.append({"xT": xs, "wu": units, "vecs": vecs, "cs": cs, "maskT": maskT, "pm": pm})
    res = run_bass_kernel_spmd(nc, in_maps, core_ids=list(range(n_cores)))
    outs = [np.asarray(r["outT"]).transpose(0, 2, 1) for r in res.results]
    return np.ascontiguousarray(np.concatenate(outs, axis=0)).astype(np.float32, copy=False)
```
